# Optimizing a Trainium2 kernel written in Bass

```python
import jax, jax.numpy as jnp
from jax import lax
import numpy as np

D_MODEL = 1024
BATCH = 4
SEQ = 4096
DEPTH = 1

HEAD_DIM = 64
HEADS_PER_GROUP = D_MODEL // 128
ATT_GROUPS = ((128, 1), (512, 4), (2048, 16))
N_ATT_GROUPS = len(ATT_GROUPS)
N_ATT_HEADS = N_ATT_GROUPS * HEADS_PER_GROUP
ATT_WIDTH = N_ATT_HEADS * HEAD_DIM
ATT_OUT_WIDTH = HEADS_PER_GROUP * HEAD_DIM
ROT_DIM = HEAD_DIM // 4
ROPE_THETA = 500000.0
Q_BLOCK = 128
POOL_WINDOWS = (2, 4, 8, 16)
POOL_GROUPS = len(POOL_WINDOWS)
POOL_GROUP_WIDTH = D_MODEL // 8
POOL_WIDTH = POOL_GROUPS * POOL_GROUP_WIDTH
IN_WIDTH = 3 * ATT_WIDTH + POOL_WIDTH + 2 * D_MODEL
D_FF = -(-8 * D_MODEL // (3 * 256)) * 256
NORM_EPS = 1e-6

kernel_name = "hybrid_dilated_attn_pool_gated_block"


def rms_norm(x, g):
    xf = x.astype(jnp.float32)
    y = xf * lax.rsqrt(jnp.mean(xf * xf, axis=-1, keepdims=True) + NORM_EPS)
    return (y * g.astype(jnp.float32)).astype(x.dtype)


def partial_rope(x, pos):
    half = ROT_DIM // 2
    inv_freq = ROPE_THETA ** (-jnp.arange(0, ROT_DIM, 2, dtype=jnp.float32) / ROT_DIM)
    ang = pos.astype(jnp.float32)[:, None] * inv_freq[None, :]
    cos = jnp.cos(ang)[None, :, None, :]
    sin = jnp.sin(ang)[None, :, None, :]
    xr = x[..., :ROT_DIM].astype(jnp.float32)
    x1, x2 = xr[..., :half], xr[..., half:]
    rot = jnp.concatenate([x1 * cos - x2 * sin, x2 * cos + x1 * sin], axis=-1)
    return jnp.concatenate([rot.astype(x.dtype), x[..., ROT_DIM:]], axis=-1)


def dilated_attention(q, k, v):
    B, S = q.shape[0], q.shape[1]
    n_blocks = S // Q_BLOCK
    scale = HEAD_DIM ** -0.5
    k_pad = [jnp.pad(k[:, :, g], ((0, 0), (w, 0), (0, 0), (0, 0))) for g, (w, _) in enumerate(ATT_GROUPS)]
    v_pad = [jnp.pad(v[:, :, g], ((0, 0), (w, 0), (0, 0), (0, 0))) for g, (w, _) in enumerate(ATT_GROUPS)]
    q_blocks = q.reshape(B, n_blocks, Q_BLOCK, N_ATT_GROUPS, HEADS_PER_GROUP, HEAD_DIM).transpose(1, 0, 2, 3, 4, 5)

    def block_fn(args):
        n, q_blk = args
        start = n * Q_BLOCK
        qi = jnp.arange(Q_BLOCK)
        outs, lses = [], []
        for g, (w, d) in enumerate(ATT_GROUPS):
            n_keys = w // d + 1
            offs = d * jnp.arange(n_keys)
            idx = qi[:, None] + w - offs[None, :]
            band_k = lax.dynamic_slice_in_dim(k_pad[g], start, Q_BLOCK + w, axis=1)
            band_v = lax.dynamic_slice_in_dim(v_pad[g], start, Q_BLOCK + w, axis=1)
            kg = band_k[:, idx]
            vg = band_v[:, idx]
            valid = ((start + qi)[:, None] - offs[None, :]) >= 0
            s = jnp.einsum('bqhd,bqkhd->bhqk', q_blk[:, :, g], kg,
                           preferred_element_type=jnp.float32) * scale
            s = jnp.where(valid, s, -jnp.inf)
            m = jnp.max(s, axis=-1, keepdims=True)
            p = jnp.exp(s - m)
            l = jnp.sum(p, axis=-1, keepdims=True)
            o = jnp.einsum('bhqk,bqkhd->bhqd', p, vg.astype(jnp.float32)) / l
            outs.append(o)
            lses.append((m + jnp.log(l))[..., 0])
        alpha = jax.nn.softmax(jnp.stack(lses, axis=0), axis=0)
        o = jnp.sum(alpha[..., None] * jnp.stack(outs, axis=0), axis=0)
        return o.transpose(0, 2, 1, 3).astype(q_blk.dtype)

    out = lax.map(block_fn, (jnp.arange(n_blocks), q_blocks))
    return out.transpose(1, 0, 2, 3, 4).reshape(B, S, ATT_OUT_WIDTH)


def multiscale_pool(z):
    B, S = z.shape[0], z.shape[1]
    zf = z.astype(jnp.float32).reshape(B, S, POOL_GROUPS, POOL_GROUP_WIDTH)
    csum = jnp.concatenate([jnp.zeros_like(zf[:, :1]), jnp.cumsum(zf, axis=1)], axis=1)
    outs = []
    for g, w in enumerate(POOL_WINDOWS):
        cg = csum[:, :, g]
        lower = jnp.concatenate([jnp.zeros_like(cg[:, :w - 1]), cg[:, :S + 1 - w]], axis=1)
        count = jnp.minimum(jnp.arange(1, S + 1), w).astype(jnp.float32)
        outs.append((cg[:, 1:] - lower) / count[None, :, None] - zf[:, :, g])
    return jnp.stack(outs, axis=2)


def setup_inputs(seed: int = 0) -> dict:
    key = jax.random.key(seed)
    ks = jax.random.split(key, 16)
    f32 = jnp.float32

    def dense(k, shape, fan_in):
        return jax.random.normal(k, shape, f32) * (fan_in ** -0.5)

    def gain(k, shape):
        return jnp.ones(shape, f32) + 0.02 * jax.random.normal(k, shape, f32)

    return {
        "x": jax.random.normal(ks[0], (BATCH, SEQ, D_MODEL), f32),
        "norm_mix": gain(ks[1], (DEPTH, D_MODEL)),
        "w_in": dense(ks[2], (DEPTH, D_MODEL, IN_WIDTH), D_MODEL),
        "w_pool_group": dense(ks[3], (DEPTH, POOL_GROUPS, POOL_GROUP_WIDTH, POOL_GROUP_WIDTH), POOL_GROUP_WIDTH),
        "pool_scale": gain(ks[4], (DEPTH, POOL_WIDTH)),
        "w_att_branch": dense(ks[5], (DEPTH, ATT_OUT_WIDTH, D_MODEL), ATT_OUT_WIDTH),
        "w_pool_branch": dense(ks[6], (DEPTH, POOL_WIDTH, D_MODEL), POOL_WIDTH),
        "w_out": dense(ks[7], (DEPTH, D_MODEL, D_MODEL), D_MODEL),
        "norm_ffn": gain(ks[8], (DEPTH, D_MODEL)),
        "w_ffn_gate": dense(ks[9], (DEPTH, D_MODEL, D_FF), D_MODEL),
        "w_ffn_up": dense(ks[10], (DEPTH, D_MODEL, D_FF), D_MODEL),
        "w_ffn_down": dense(ks[11], (DEPTH, D_FF, D_MODEL), D_FF),
        "norm_final": gain(ks[12], (D_MODEL,)),
    }


def reference(x, norm_mix, w_in, w_pool_group, pool_scale, w_att_branch, w_pool_branch,
              w_out, norm_ffn, w_ffn_gate, w_ffn_up, w_ffn_down, norm_final):
    B, S, _ = x.shape
    pos = jnp.arange(S)
    h = x
    for l in range(DEPTH):
        u = rms_norm(h, norm_mix[l])
        proj = u @ w_in[l]
        q, k, v, z, g_att, g_pool = jnp.split(
            proj, np.cumsum([ATT_WIDTH, ATT_WIDTH, ATT_WIDTH, POOL_WIDTH, D_MODEL]).tolist(), axis=-1)
        q = partial_rope(q.reshape(B, S, N_ATT_HEADS, HEAD_DIM), pos)
        k = partial_rope(k.reshape(B, S, N_ATT_HEADS, HEAD_DIM), pos)
        grp = (B, S, N_ATT_GROUPS, HEADS_PER_GROUP, HEAD_DIM)
        att = dilated_attention(q.reshape(grp), k.reshape(grp), v.reshape(grp))
        pooled = multiscale_pool(z)
        pooled = jnp.einsum('bsgc,gce->bsge', pooled, w_pool_group[l].astype(jnp.float32))
        pooled = (pooled.reshape(B, S, POOL_WIDTH) * pool_scale[l].astype(jnp.float32)).astype(h.dtype)
        y_att = att @ w_att_branch[l]
        y_pool = pooled @ w_pool_branch[l]
        merged = jax.nn.sigmoid(g_att) * y_att + jax.nn.sigmoid(g_pool) * y_pool
        h = h + merged @ w_out[l]
        f = rms_norm(h, norm_ffn[l])
        h = h + (jax.nn.silu(f @ w_ffn_gate[l]) * (f @ w_ffn_up[l])) @ w_ffn_down[l]
    return rms_norm(h, norm_final)
```

```python
import numpy as np
import concourse.bass as bass
import concourse.mybir as mybir
from concourse.bass_utils import run_bass_kernel_spmd
from contextlib import ExitStack

F32 = mybir.dt.float32
BF16 = mybir.dt.bfloat16
ALU = mybir.AluOpType
AF = mybir.ActivationFunctionType

D = 1024
S = 4096
NT = 2048
DFF = 2816
NJ = 22
EPS = 1e-6
GROUPS = ((128, 1), (512, 4), (2048, 16))
NBLK = (17, 20, 32)
KT_OFF = (0, 2176, 4736)
VS_OFF = (0, 17 * 192, 37 * 192)

NW = 52512
UT_H, UT_O, ATT = 0, 8192, 16384
CST, MSK, PRM, ONE = 20480, 20608, 21120, 21184
R = 21504
TAB, QT, KT, VS, ACC, WP, PT, QRAW, T12 = R, 25600, 28672, 33088, 39712, 43808, 48416, 49440, 49952
XS, SQ, RS = 25600, 33792, 35840
POOLT, MERGED, TMPB = 29696, 33792, 41984
ACTT, RS2 = 8192, 19456
FT, WGU, WD, SQ2, SIL, OST = 29696, 33792, 39936, 46080, 48128, 49152


class Tok:
    __slots__ = ("sem", "val")

    def __init__(self, sem, val):
        self.sem, self.val = sem, val


class DSem:
    def __init__(self, sem):
        self.sem, self.cnt = sem, 0


class Prog:
    ENGS = ("pe", "act", "dve", "pool", "sp")
    CE = ("pe", "act", "dve")

    def __init__(self, nc, es):
        self.nc, self.es = nc, es
        self.ops = {e: [] for e in self.ENGS}
        self.esem = {e: es.enter_context(nc.semaphore("es_" + e)) for e in self.CE}
        self.ecnt = {e: 0 for e in self.CE}
        self.seen = {e: {} for e in self.ENGS}
        self.nsem = 0
        self.dma_out = []

    def dsem(self):
        self.nsem += 1
        return DSem(self.es.enter_context(self.nc.semaphore(f"ds{self.nsem}")))

    def _waits(self, eng, after):
        waits = []
        for t in after:
            if t is None:
                continue
            k = id(t.sem)
            if self.seen[eng].get(k, 0) >= t.val:
                continue
            self.seen[eng][k] = t.val
            waits.append((t.sem, t.val))
        return waits

    def op(self, eng, fn, after=()):
        waits = self._waits(eng, after)
        self.ecnt[eng] += 1
        self.ops[eng].append((waits, fn, (self.esem[eng], 1)))
        return Tok(self.esem[eng], self.ecnt[eng])

    def dma(self, q, out, in_, ds, after=()):
        waits = self._waits(q, after)
        ds.cnt += 16
        self.ops[q].append((waits, lambda e: e.dma_start(out=out, in_=in_), (ds.sem, 16)))
        t = Tok(ds.sem, ds.cnt)
        self.dma_out.append(t)
        return t

    def wait(self, eng, after):
        waits = self._waits(eng, after)
        if waits:
            self.ops[eng].append((waits, None, None))

    def barrier(self):
        best = {}
        for t in self.dma_out:
            if id(t.sem) not in best or best[id(t.sem)].val < t.val:
                best[id(t.sem)] = t
        toks = [Tok(self.esem[e], self.ecnt[e]) for e in self.CE if self.ecnt[e] > 0] + list(best.values())
        self.dma_out = []
        for e in self.ENGS:
            self.wait(e, toks)

    def emit(self):
        with self.nc.Block() as block:
            for e, deco in (("pe", block.tensor), ("act", block.scalar), ("dve", block.vector),
                            ("pool", block.gpsimd), ("sp", block.sync)):
                ops = self.ops[e]

                def body(eng, ops=ops):
                    for waits, fn, inc in ops:
                        for s, v in waits:
                            eng.wait_ge(s, v)
                        if fn is not None:
                            ins = fn(eng)
                            if inc is not None:
                                ins.then_inc(inc[0], inc[1])

                deco(body)


def cm_view(ap, g, m):
    if g == 0:
        return ap[:, 512 * m:512 * m + 512]
    if g == 1:
        return ap[:, m:2048:4]
    return ap.rearrange("p (i r) -> p r i", r=16)[:, 4 * m:4 * m + 4, :]


def cont_view(ap, g):
    if g == 2:
        return ap.rearrange("p (r i) -> p r i", r=4)
    return ap


class _Stop(Exception):
    pass


DEBUG = {"stop": None, "dump": (0, 16)}


def build_program():
    nc = bass.Bass("TRN2", target_bir_lowering=False)
    dr = lambda n, s, k="ExternalInput": nc.dram_tensor(n, list(s), F32, kind=k).ap()
    xT = dr("xT", [D, 4096])
    w_in = dr("w_in", [D, 7168])
    tab = dr("tab", [128, 8192])
    cst = dr("cst", [128, 128])
    msk = dr("msk", [128, 1024])
    prm = dr("prm", [128, 128])
    wpg = dr("wpg", [128, 512])
    wab = dr("wab", [512, D])
    wpb = dr("wpb", [512, D])
    wout = dr("wout", [D, D])
    wgu = dr("wgu", [D, 2 * DFF])
    wd = dr("wd", [DFF, D])
    oT = dr("oT", [D, NT], "ExternalOutput")
    if DEBUG["stop"] is not None:
        dbg = dr("dbg", [128, DEBUG["dump"][1]], "ExternalOutput")
    else:
        dbg = None

    with ExitStack() as es:
        P = Prog(nc, es)
        try:
            _body(nc, es, P, locals())
        except _Stop:
            pass
        P.emit()
    return nc


def _body(nc, es, P, L):
    xT, w_in, tab, cst, msk, prm, wpg, wab, wpb, wout, wgu, wd, oT = [
        L[k] for k in ("xT", "w_in", "tab", "cst", "msk", "prm", "wpg", "wab", "wpb", "wout", "wgu", "wd", "oT")]
    dbg = L["dbg"]

    def chk(name):
        if DEBUG["stop"] == name:
            P.barrier()
            o, n = DEBUG["dump"]
            t = P.dma("sp", dbg, AR[:, o:o + n], P.dsem())
            P.wait("sp", [t])
            raise _Stop()
    if True:
        AR = es.enter_context(nc.sbuf_tensor("arena", [128, NW], F32))
        PS_ALL = es.enter_context(nc.psum_tensor("ps_all", [128, 4096], F32))
        PSB = [PS_ALL[:, i * 512:(i + 1) * 512] for i in range(8)]

        def f32(off, n):
            return AR[:, off:off + n]

        def bf(off, n):
            return AR[:, off:off + n // 2].bitcast(BF16)

        def mm(out, lhsT, rhs, start, stop, after=()):
            return P.op("pe", lambda e: e.matmul(out, lhsT=lhsT, rhs=rhs, start=start, stop=stop), after)

        def act(out, in_, func, after=(), **kw):
            return P.op("act", lambda e: e.activation(out=out, in_=in_, func=func, **kw), after)

        def tt(out, in0, in1, op, after=()):
            return P.op("dve", lambda e: e.tensor_tensor(out=out, in0=in0, in1=in1, op=op), after)

        def stt(out, in0, scalar, in1, op0, op1, after=()):
            return P.op("dve", lambda e: e.scalar_tensor_tensor(out=out, in0=in0, scalar=scalar, in1=in1,
                                                                op0=op0, op1=op1), after)

        uth = bf(UT_H, 8 * 2048).rearrange("p (c t) -> p c t", c=8)
        uto = bf(UT_O, 8 * 2048).rearrange("p (c t) -> p c t", c=8)
        att = bf(ATT, 4 * 2048).rearrange("p (c t) -> p c t", c=4)
        cstv = f32(CST, 128)
        mskv = bf(MSK, 1024)
        prmv = bf(PRM, 128)
        onev = bf(ONE, 128)
        tabv = bf(TAB, 8192)
        tcos_h, tcos_o = tabv[:, 0:2048], tabv[:, 2048:4096]
        tsin_h, tsin_o = tabv[:, 4096:6144], tabv[:, 6144:8192]

        t_cst = P.dma("sp", cstv, cst, P.dsem())
        t_msk = P.dma("pool", mskv, msk, P.dsem())
        t_prm = P.dma("pool", prmv, prm, P.dsem())
        t_ms1 = P.op("dve", lambda e: e.memset(tabv[:, 0:4096], 1.0))
        t_ms2 = P.op("dve", lambda e: e.memset(tabv[:, 4096:8192], 0.0))
        d_tab = P.dsem()
        for p0 in (0, 64):
            t_tab = P.dma("pool", tabv[p0:p0 + 16, :].rearrange("p (a b) -> p a b", b=1024),
                          tab[p0:p0 + 16, :].rearrange("p (a b) -> p a b", b=1024), d_tab, after=[t_ms1, t_ms2])
        t_one = P.op("dve", lambda e: e.memset(onev, 1.0))
        t_const = [t_cst, t_msk, t_prm, t_tab, t_one]

        def rms_stats(src_fn, sqv, rsv, psb, after, rs_after=()):
            ts = []
            after = list(after) + [Tok(P.esem["pe"], P.ecnt["pe"])]
            for c in range(8):
                ts.append(act(sqv[:, c, :], src_fn(c), AF.Square, after=after))
            t = None
            for c in range(8):
                t = mm(psb[:], onev, sqv[:, c, :], c == 0, c == 7, after=[ts[c], t_one] + list(after))
            t = act(rsv, psb[:], AF.Ln, after=[t, t_cst] + list(rs_after), scale=1.0 / D, bias=cstv[:, 96:97])
            t = act(rsv, rsv, AF.Exp, after=[t], scale=-0.5)
            return t

        xTv = xT.rearrange("(c p) t -> p c t", p=128)
        xs = [f32(XS + i * 4096, 4096).rearrange("p (c t) -> p c t", c=8) for i in range(3)]
        sqv2 = [bf(37888 + j * 2048, 4096).rearrange("p (c t) -> p c t", c=8) for j in range(2)]
        rsv2 = [f32(41984, 512), f32(42496, 512)]
        d_x = [P.dsem(), P.dsem(), P.dsem()]
        xs_free = [[], [], []]
        rs_free = [[], []]
        for tg in range(8):
            i = tg % 3
            j = tg % 2
            t_ld = P.dma("sp", xs[i], xTv[:, :, tg * 512:(tg + 1) * 512], d_x[i], after=xs_free[i])
            t_rs = rms_stats(lambda c: xs[i][:, c, :], sqv2[j], rsv2[j], PSB[tg % 2], [t_ld], rs_after=rs_free[j])
            dst = uth if tg < 4 else uto
            col = (tg % 4) * 512
            tl = []
            for c in range(8):
                tl.append(stt(dst[:, c, col:col + 512], xs[i][:, c, :], cstv[:, c:c + 1], rsv2[j],
                              ALU.mult, ALU.mult, after=[t_rs, t_cst]))
            xs_free[i] = [tl[-1]]
            rs_free[j] = [tl[-1]]
        P.barrier()
        chk("P0")

        qtv = [bf(QT + g * 1024, 2048) for g in range(3)]
        ktv = [bf(KT, 8832)[:, KT_OFF[g]:KT_OFF[g] + NBLK[g] * 128] for g in range(3)]
        vsv = [bf(VS, 69 * 192)[:, VS_OFF[g]:VS_OFF[g] + NBLK[g] * 192].rearrange("p (b k) -> p b k", k=192)
               for g in range(3)]
        accv = [f32(ACC + e * 2048, 2048) for e in range(2)]
        wpv = bf(WP, 8 * 1152).rearrange("p (c n) -> p c n", c=8)
        ptv = [bf(PT + i * 256, 512) for i in range(4)] + [bf(52000 + i * 256, 512) for i in range(2)]
        qrawv = [bf(QRAW + i * 256, 512) for i in range(2)]
        t1v = [f32(T12 + i * 512, 512) for i in range(2)]
        t2v = [f32(T12 + 1024 + i * 512, 512) for i in range(2)]
        ntmp = f32(T12, 2048)
        w_inv = w_in.rearrange("(c p) n -> p c n", p=128)
        d_wp = P.dsem()

        vs_all = bf(VS, 69 * 192).rearrange("p (b k) -> p b k", k=192)
        P.op("dve", lambda e: e.memset(vs_all[:, :, 64:128], 1.0))
        t_wp = P.dma("pool", wpv, w_inv[:, :, 0:1152], d_wp)

        XS_ = dict(psq=[None, None], pss=[None, None], v=[None] * 4, t1=[None, None], tpe=None)
        for sp in range(4):
            units = []

            def cmdst(region, d, m):
                if d == 1:
                    return region[:, 512 * m:512 * m + 512]
                w = 512 // d
                return region.rearrange("p (r i) -> p r i", r=d)[:, :, w * m:w * m + w]

            for g in range(3):
                d = GROUPS[g][1]
                for typ in (0, 1):
                    wcol = (g * 3 + typ) * 128
                    dstb = qtv[g] if typ == 0 else ktv[g][:, 0:2048]
                    for m in range(4):
                        units.append((cmdst(dstb, d, m), wcol,
                                      (lambda c, m=m: uto[:, c, 512 * m:512 * m + 512]),
                                      tcos_o[:, 512 * m:512 * m + 512], tsin_o[:, 512 * m:512 * m + 512], 512, d))
                wcol = (g * 3 + 1) * 128
                if g == 2:
                    for m in range(4):
                        units.append((cmdst(ktv[g][:, 2048:4096], 16, m), wcol,
                                      (lambda c, m=m: uth[:, c, 512 * m:512 * m + 512]),
                                      tcos_h[:, 512 * m:512 * m + 512], tsin_h[:, 512 * m:512 * m + 512], 512, 16))
                elif g == 1:
                    units.append((ktv[g][:, 2048:2560].rearrange("p (r i) -> p r i", r=4), wcol,
                                  (lambda c: uth[:, c, 1536:2048]), tcos_h[:, 1536:2048], tsin_h[:, 1536:2048], 512, 4))
                else:
                    units.append((ktv[g][:, 2048:2176], wcol, (lambda c: uth[:, c, 1920:2048]),
                                  tcos_h[:, 1920:2048], tsin_h[:, 1920:2048], 128, 1))

            P.wait("pe", [t_wp])
            chk(f"A{sp}Pw")
            if DEBUG.get("units") is not None:
                units = [units[k] for k in DEBUG["units"]]
            v_batches = []
            for g in range(3):
                d = GROUPS[g][1]
                blocks = []
                for b in range(16):
                    r, qb = divmod(b, 16 // d)
                    blocks.append((b, False, r + d * 128 * qb))
                for r in range(d):
                    blocks.append((16 + r, True, 2048 - 128 * d + r))
                for b0 in range(0, len(blocks), 4):
                    v_batches.append((g, blocks[b0:b0 + 4]))
            vb_i = [0]
            vfree = list(XS_["v"])

            def emit_vbatch():
                vbank = vb_i[0]
                g, batch = v_batches[vbank]
                vb_i[0] += 1
                d = GROUPS[g][1]
                wcol = (g * 3 + 2) * 128
                nb = len(batch)
                psv = PSB[4 + vbank % 4]
                t = None
                for k, (b, hist, s0) in enumerate(batch):
                    src = uth if hist else uto
                    for c in range(8):
                        t = mm(psv[:, k * 128:(k + 1) * 128], src[:, c, s0:s0 + 127 * d + 1:d],
                               wpv[:, c, wcol:wcol + 128], c == 0, c == 7, after=[t_wp, vfree[vbank % 4]])
                bfirst = batch[0][0]
                outv = vsv[g][:, bfirst:bfirst + nb, :].rearrange("p b (h k) -> p b h k", k=64)[:, :, 0:3:2, :]
                inv = psv[:, 0:nb * 128].rearrange("p (b h k) -> p b h k", b=nb, h=2)
                vfree[vbank % 4] = act(outv, inv, AF.Copy, after=[t, XS_["tpe"]])

            psq_free = list(XS_["psq"])
            pss_free = list(XS_["pss"])
            qraw_free = [None, None]
            t1_free = list(XS_["t1"])
            pend = None

            def do_perm(pd):
                (ui, t_raw, i, N, shp, dst, cosv, sinv, t_t1) = pd
                pss = PSB[2 + i]
                LV = DEBUG.get("lv", 9)
                if LV < 3:
                    return
                t_p = mm(pss[:, 0:N], prmv, qrawv[i][:, 0:N], True, True, after=[t_raw, pss_free[i], t_prm])
                if LV < 4:
                    return
                t_t2 = tt(t2v[i][:, 0:N], pss[:, 0:N], sinv, ALU.mult, after=[t_p, t_tab, t1_free[i]])
                if LV < 5:
                    return
                t_o = tt(dst, shp(t1v[i][:, 0:N]), shp(t2v[i][:, 0:N]), ALU.add, after=[t_t1, t_t2, XS_["tpe"]])
                pss_free[i] = t_t2
                qraw_free[i] = t_p
                t1_free[i] = t_o

            for ui, (dst, wcol, rhs_fn, cosv, sinv, N, dd) in enumerate(units):
                i = ui % 2
                psq = PSB[i]
                shp = (lambda a, dd=dd: a.rearrange("p (i r) -> p r i", r=dd)) if dd > 1 else (lambda a: a)
                t = None
                for c in range(8):
                    t = mm(psq[:, 0:N], wpv[:, c, wcol:wcol + 128], rhs_fn(c), c == 0, c == 7,
                           after=[t_wp, psq_free[i]])
                t_raw = act(qrawv[i][:, 0:N], psq[:, 0:N], AF.Copy, after=[t, qraw_free[i]])
                t_t1 = tt(t1v[i][:, 0:N], psq[:, 0:N], cosv, ALU.mult, after=[t, t_raw, t1_free[i], t_tab])
                psq_free[i] = t_t1
                if pend is not None:
                    do_perm(pend)
                pend = (ui, t_raw, i, N, shp, dst, cosv, sinv, t_t1)
                while vb_i[0] < len(v_batches) and vb_i[0] * len(units) < (ui + 1) * len(v_batches):
                    emit_vbatch()
            do_perm(pend)
            chk(f"A{sp}Pqk")

            while vb_i[0] < len(v_batches):
                emit_vbatch()
            t_Pdone = [Tok(P.esem["dve"], P.ecnt["dve"]), Tok(P.esem["act"], P.ecnt["act"])]
            t_Ppe = Tok(P.esem["pe"], P.ecnt["pe"])
            chk(f"A{sp}P")
            if sp < 3:
                t_wp = P.dma("pool", wpv, w_inv[:, :, (sp + 1) * 1152:(sp + 2) * 1152], d_wp, after=[t_Ppe])
            else:
                zw = bf(WP, 4096).rearrange("p (c n) -> p c n", c=8)
                wpgv = bf(WP + 2048, 512)
                d_b = P.dsem()
                P.dma("pool", zw, w_inv[:, :, 4608:5120], d_b, after=[t_Ppe])
                t_wpg = P.dma("pool", wpgv, wpg, d_b)
                t_zw = t_wpg

            tiles = []
            nbatch = 0
            for m in range(4):
                lst = []
                for k in range(4):
                    b = 4 * m + k
                    lst.append((0, b, (b - 1) if b >= 1 else 16, b >= 1, slice(k * 128, k * 128 + 128)))
                for r4 in range(4):
                    b = r4 * 4 + m
                    lst.append((1, b, (b - 1) if m >= 1 else 16 + r4, m >= 1, slice(r4, 512, 4)))
                for j, (g_, b_, pb_, nm_, oc_) in enumerate(lst):
                    tiles.append((g_, b_, pb_, nm_, nbatch % 2, oc_, j == 0, j == len(lst) - 1, ("copy", m)))
                nbatch += 1
            for m in range(4):
                for k in range(4):
                    b = 4 * m + k
                    tiles.append((2, b, 16 + b, False, nbatch % 2, slice(k * 128, k * 128 + 128), k == 0, k == 3,
                                  ("add2", m)))
                nbatch += 1
            sfree = [[psq_free[0], psq_free[1]], [pss_free[0], pss_free[1]]]
            ptfree = [None] * 6
            ofree = [[vfree[0], vfree[1]], [vfree[2], vfree[3]]]
            pend_pv = []
            acc_last = [None, None]

            def do_pv(item):
                (ti, tile, t_m) = item
                (g, b, prevb, normal, par, oc, first, last, evac) = tile
                i4 = ti % 6
                tl = None
                for e in range(2):
                    pso = PSB[4 + 2 * e + par]
                    o = pso[:, oc]
                    P.op("pe", lambda en, o=o, l=vsv[g][:, prevb, e * 64:e * 64 + 128], r=ptv[i4][:, e * 256:e * 256 + 128],
                         st=first: en.matmul(o, lhsT=l, rhs=r, start=st, stop=False, skip_group_check=True),
                         [t_m, ofree[e][par]])
                    tl = P.op("pe", lambda en, o=o, l=vsv[g][:, b, e * 64:e * 64 + 128],
                              r=ptv[i4][:, e * 256 + 128:e * 256 + 256], sp_=last:
                              en.matmul(o, lhsT=l, rhs=r, start=False, stop=sp_, skip_group_check=True))
                ptfree[i4] = tl
                if last:
                    kind, m = evac
                    for e in range(2):
                        pso = PSB[4 + 2 * e + par]
                        if kind == "copy":
                            ofree[e][par] = act(accv[e][:, 512 * m:512 * m + 512], pso[:], AF.Copy,
                                                after=[tl, XS_["t1"][0]])
                        else:
                            av = cm_view(accv[e], 2, m)
                            ofree[e][par] = tt(av, cont_view(pso[:], 2), av, ALU.add, after=[tl, acc_last[e]])
                        acc_last[e] = ofree[e][par]

            for ti, tile in enumerate(tiles):
                (g, b, prevb, normal) = tile[0:4]
                i4 = ti % 6
                i2 = ti % 2
                t = None
                for kb_, c0_ in ((prevb, 0), (b, 128)):
                    for e in range(2):
                        ps = PSB[2 * i2 + e]
                        pr = slice(64 * e, 64 * e + 64)
                        q = qtv[g][pr, b * 128:(b + 1) * 128]
                        t = mm(ps[:, c0_:c0_ + 128], ktv[g][pr, kb_ * 128:(kb_ + 1) * 128], q, True, True,
                               after=list(sfree[i2]) + t_Pdone)
                ps2 = PS_ALL[:, 2 * i2 * 512:(2 * i2 + 2) * 512].rearrange("p (e c) -> p e c", e=2)[:, :, 0:256]
                t_e = act(ptv[i4].rearrange("p (e c) -> p e c", e=2), ps2, AF.Exp, after=[t, ptfree[i4]], scale=0.125)
                sfree[i2] = [t_e]
                mv = mskv[:, 0:512] if normal else mskv[:, 512:1024]
                t_m = tt(ptv[i4], ptv[i4], mv, ALU.mult, after=[t_e, t_msk])
                pend_pv.append((ti, tile, t_m))
                if len(pend_pv) > 4:
                    do_pv(pend_pv.pop(0))
            while pend_pv:
                do_pv(pend_pv.pop(0))
            tU = Tok(P.esem["dve"], P.ecnt["dve"])
            tA = Tok(P.esem["act"], P.ecnt["act"])
            t_c0 = act(ntmp[0:64, :], accv[0][64:128, :], AF.Ln, after=[tU, tA])
            t_c1 = act(ntmp[64:128, :], accv[1][0:64, :], AF.Ln, after=[tU, tA])
            t_l = act(ntmp, ntmp, AF.Exp, after=[t_c0, t_c1], scale=-1.0)
            tt(att[0:64, sp, :], accv[0][0:64, :], ntmp[0:64, :], ALU.mult, after=[t_l])
            t_nrm = tt(att[64:128, sp, :], accv[1][64:128, :], ntmp[64:128, :], ALU.mult, after=[t_l])
            if sp == 3:
                P.barrier()
            else:
                XS_["psq"] = [sfree[0][0], sfree[0][0]]
                XS_["pss"] = [sfree[1][0], sfree[1][0]]
                XS_["v"] = [ofree[0][0], ofree[0][1], ofree[1][0], ofree[1][1]]
                XS_["t1"] = [t_nrm, t_nrm]
                XS_["tpe"] = Tok(P.esem["pe"], P.ecnt["pe"])
            chk(f"A{sp}")

        ht = [f32(UT_H + c * 2048, 2048) if c < 4 else f32(R + (c - 4) * 2048, 2048) for c in range(8)]
        d_h = P.dsem()
        t_h = [None] * 8
        for c in range(4, 8):
            t_h[c] = P.dma("sp", ht[c], xT[c * 128:(c + 1) * 128, 2048:4096], d_h)
        poolt = bf(POOLT, 4 * 2048).rearrange("p (c t) -> p c t", c=4)
        merged = bf(MERGED, 8 * 2048).rearrange("p (c t) -> p c t", c=8)
        pre = [bf(TMPB + i * 256, 512) for i in range(4)]
        o = WP + 2048 + 256
        zbuf = [[f32(o + (2 * i + k) * 528, 528) for k in range(2)] for i in range(4)]; o += 4224
        ta, tb = f32(o, 528), f32(o + 528, 528); o += 1056
        t16 = f32(o, 16); o += 16
        assert o <= NW
        halo_tok = [None] * 4
        psz_free = [None] * 4
        for gz in range(4):
            t = None
            for c in range(8):
                t = mm(PSB[gz][:, 0:16], zw[:, c, gz * 128:(gz + 1) * 128], uth[:, c, 2032:2048], c == 0, c == 7,
                       after=[t_zw])
            halo_tok[gz] = act(zbuf[gz][0][:, 0:16], PSB[gz][:, 0:16], AF.Copy, after=[t])
            psz_free[gz] = halo_tok[gz]
        t_halo = Tok(P.esem["pe"], P.ecnt["pe"])
        for c in range(4):
            t_h[c] = P.dma("sp", ht[c], xT[c * 128:(c + 1) * 128, 2048:4096], d_h, after=[t_halo])
        t_h = [t_h[3]] * 8
        pre_free = [None] * 4
        ppfree = [None] * 4
        zb_free = [[None, None] for _ in range(4)]
        b1_pend = []
        tab_last = [None]

        def b1_pool(item):
            gz, cols, t_p = item
            psp = PSB[4 + gz]
            t_m = mm(psp[:], wpgv[:, gz * 128:(gz + 1) * 128], pre[gz], True, True,
                     after=[t_p, t_wpg, ppfree[gz]])
            pre_free[gz] = t_m
            ppfree[gz] = act(poolt[:, gz, cols], psp[:], AF.Copy, after=[t_m], scale=cstv[:, 24 + gz:25 + gz])
        for tg in range(4):
            cols = slice(tg * 512, tg * 512 + 512)
            kb = tg % 2
            for gz in range(4):
                psz = PSB[gz]
                z = zbuf[gz][kb]
                zn = zbuf[gz][1 - kb]
                t = None
                for c in range(8):
                    t = mm(psz[:], zw[:, c, gz * 128:(gz + 1) * 128], uto[:, c, cols], c == 0, c == 7,
                           after=[t_zw, psz_free[gz]])
                t_z = act(z[:, 16:528], psz[:], AF.Copy, after=[t, zb_free[gz][kb]])
                psz_free[gz] = t_z
                t_hn = act(zn[:, 0:16], z[:, 512:528], AF.Copy, after=[t_z, zb_free[gz][1 - kb]])
                w = 2 ** (gz + 1)
                cur, lo, t_c = z, 0, t_z
                bufs = [ta, tb]
                sh = 1
                for lvl in range(gz + 1):
                    nxt = bufs[lvl % 2]
                    nlo = lo + sh
                    t_c = tt(nxt[:, nlo:528], cur[:, nlo:528], cur[:, nlo - sh:528 - sh], ALU.add,
                             after=[t_c, halo_tok[gz], pre_free[gz], tab_last[0]] if lvl == 0 else [t_c])
                    cur, lo = nxt, nlo
                    sh *= 2
                t_p = stt(pre[gz], cur[:, 16:528], 1.0 / w, z[:, 16:528], ALU.mult, ALU.subtract,
                          after=[t_c, pre_free[gz]])
                if tg == 0:
                    t_f = tt(t16, cur[:, 16:32], cstv[:, 28 + gz * 16:28 + gz * 16 + 16], ALU.mult, after=[t_p])
                    t_p = tt(pre[gz][:, 0:16], t16, z[:, 16:32], ALU.subtract, after=[t_f])
                zb_free[gz][kb] = t_p
                tab_last[0] = t_p
                halo_tok[gz] = t_hn
                b1_pend.append((gz, cols, t_p))
                if len(b1_pend) > 2:
                    b1_pool(b1_pend.pop(0))
        while b1_pend:
            b1_pool(b1_pend.pop(0))
        P.barrier()
        chk("B1")
        o = TMPB
        ring = []
        for i in range(2):
            ga = bf(o, 1024).rearrange("p (c n) -> p c n", c=8); o += 512
            gp = bf(o, 1024).rearrange("p (c n) -> p c n", c=8); o += 512
            ab = bf(o, 512).rearrange("p (c n) -> p c n", c=4); o += 256
            pb = bf(o, 512).rearrange("p (c n) -> p c n", c=4); o += 256
            ring.append((ga, gp, ab, pb))
        sig = [[f32(o + (2 * i + k) * 512, 512) for k in range(2)] for i in range(2)]; o += 2048
        m12 = [[f32(o + (2 * i + k) * 512, 512) for k in range(2)] for i in range(2)]; o += 2048
        wabv = wab.rearrange("(c p) n -> p c n", p=128)
        wpbv = wpb.rearrange("(c p) n -> p c n", p=128)
        d_r = [P.dsem(), P.dsem()]
        rfree = [[], []]
        sgfree = [None, None]
        mfree = [None, None]
        pfree = [[None] * 4, [None] * 4]
        it = 0
        for dc in range(8):
            i = dc % 2
            ga, gp, ab, pb = ring[i]
            cs = slice(dc * 128, dc * 128 + 128)
            P.dma("pool", ga, w_inv[:, :, 5120 + dc * 128:5120 + dc * 128 + 128], d_r[i], after=rfree[i])
            P.dma("pool", gp, w_inv[:, :, 6144 + dc * 128:6144 + dc * 128 + 128], d_r[i])
            P.dma("pool", ab, wabv[:, :, cs], d_r[i])
            t_w = P.dma("pool", pb, wpbv[:, :, cs], d_r[i])
            for tg in range(4):
                cols = slice(tg * 512, tg * 512 + 512)
                s = it % 2
                it += 1
                bA, bP, bYA, bYP = [PSB[4 * s + k] for k in range(4)]
                tA = tP = tYA = tYP = None
                for c in range(8):
                    tA = mm(bA[:], ga[:, c, :], uto[:, c, cols], c == 0, c == 7, after=[t_w, pfree[s][0]])
                for c in range(8):
                    tP = mm(bP[:], gp[:, c, :], uto[:, c, cols], c == 0, c == 7, after=[pfree[s][1]])
                for c in range(4):
                    tYA = mm(bYA[:], ab[:, c, :], att[:, c, cols], c == 0, c == 3, after=[pfree[s][2]])
                for c in range(4):
                    tYP = mm(bYP[:], pb[:, c, :], poolt[:, c, cols], c == 0, c == 3, after=[pfree[s][3]])
                t_sa = act(sig[s][0], bA[:], AF.Sigmoid, after=[tA, sgfree[s]])
                t_sp = act(sig[s][1], bP[:], AF.Sigmoid, after=[tP, sgfree[s]])
                pfree[s][0], pfree[s][1] = t_sa, t_sp
                t_m1 = tt(m12[s][0], bYA[:], sig[s][0], ALU.mult, after=[tYA, t_sa, mfree[s]])
                t_m2 = tt(m12[s][1], bYP[:], sig[s][1], ALU.mult, after=[tYP, t_sp, mfree[s]])
                pfree[s][2], pfree[s][3] = t_m1, t_m2
                sgfree[s] = t_m2
                t_mg = tt(merged[:, dc, cols], m12[s][0], m12[s][1], ALU.add, after=[t_m1, t_m2])
                mfree[s] = t_mg
            rfree[i] = [Tok(P.esem["pe"], P.ecnt["pe"])]
        P.barrier()
        chk("B2")
        actt = bf(ACTT, NJ * 1024).rearrange("p (j t) -> p j t", j=NJ)
        rs2 = [f32(RS2, 512), f32(RS2 + 512, 512)]
        ftv = bf(FT, 8 * 1024).rearrange("p (c t) -> p c t", c=8)
        wguv = [bf(WGU + i * 2048, 4096).rearrange("p (c n) -> p c n", c=8) for i in range(3)]
        wdv = [bf(WD + i * 2816, 5632).rearrange("p (j n) -> p j n", j=NJ) for i in range(2)]
        sq2 = bf(SQ2, 4096).rearrange("p (c t) -> p c t", c=8)
        sil = [f32(SIL + i * 512, 512) for i in range(2)]
        ost = [f32(OST + i * 512, 512) for i in range(4)]
        wgu_d = wgu.rearrange("(c p) n -> p c n", p=128)
        wd_d = wd.rearrange("(j p) n -> p j n", p=128)
        d_g = [P.dsem() for _ in range(3)]
        d_d = [P.dsem() for _ in range(2)]
        d_out = [P.dsem() for _ in range(4)]
        gfree = [[], [], []]
        dfree = [[], []]
        ostfree = [None] * 4
        out_toks = []
        C = dict(gi=0, di=0, oi=0, ofree2=[None] * 4)

        def norm_a(src_fn, after):
            return [act(sq2[:, c, :], src_fn(c), AF.Square, after=after) for c in range(8)]

        def norm_b(ts, psb, rsv):
            t = None
            for c in range(8):
                t = mm(psb[:], onev, sq2[:, c, :], c == 0, c == 7, after=[ts[c], t_one])
            t_ln = act(rsv, psb[:], AF.Ln, after=[t, t_cst], scale=1.0 / D, bias=cstv[:, 96:97])
            t_ex = act(rsv, rsv, AF.Exp, after=[t_ln], scale=-0.5)
            return t, t_ln, t_ex

        def all_toks():
            return [Tok(P.esem[e], P.ecnt[e]) for e in ("pe", "act", "dve")]

        def make_c1(pp, first_after, ft_after):
            st = {}

            def src(tl_):
                tg = 2 * pp + tl_
                return lambda c: ht[c][:, tg * 512:tg * 512 + 512]

            def fin(tl_, t_ex):
                tg = 2 * pp + tl_
                for c in range(8):
                    st["t_ft"] = stt(ftv[:, c, tl_ * 512:tl_ * 512 + 512], ht[c][:, tg * 512:tg * 512 + 512],
                                     cstv[:, 8 + c:9 + c], rs2[tl_], ALU.mult, ALU.mult, after=[t_ex] + list(ft_after))

            def s0():
                st["ts"] = norm_a(src(0), first_after() if callable(first_after) else first_after)

            def s1():
                t_mm, t_ln, t_ex = norm_b(st["ts"], PSB[0], rs2[0])
                fin(0, t_ex)
                st["ts"] = norm_a(src(1), [t_mm, Tok(P.esem["dve"], P.ecnt["dve"])])

            def s2():
                t_mm, t_ln, t_ex = norm_b(st["ts"], PSB[1], rs2[1])
                fin(1, t_ex)
            return [s0, s1, s2], st

        def make_c4(pp):
            st = {}

            def src(tl_):
                tg = 2 * pp + tl_
                return lambda c: ht[c][:, tg * 512:tg * 512 + 512]

            def fin(tl_, t_ex):
                tg = 2 * pp + tl_
                cols = slice(tg * 512, tg * 512 + 512)
                for c in range(8):
                    k = C["oi"] % 4
                    C["oi"] += 1
                    t_o = stt(ost[k], ht[c][:, cols], cstv[:, 16 + c:17 + c], rs2[tl_], ALU.mult, ALU.mult,
                              after=[t_ex, ostfree[k]])
                    ostfree[k] = P.dma("sp", oT[c * 128:(c + 1) * 128, cols], ost[k], d_out[k], after=[t_o])
                    out_toks.append(ostfree[k])

            def s0():
                st["ts"] = norm_a(src(0), all_toks())

            def s1():
                t_mm, t_ln, t_ex = norm_b(st["ts"], PSB[4], rs2[0])
                C["ofree2"][0] = t_ln
                fin(0, t_ex)
                st["ts"] = norm_a(src(1), [t_mm])

            def s2():
                t_mm, t_ln, t_ex = norm_b(st["ts"], PSB[5], rs2[1])
                C["ofree2"][1] = t_ln
                fin(1, t_ex)
            return [s0, s1, s2]

        def c2(pp, t_ft, hooks):
            gufree = [[None, None], [None, None]]
            silfree = [None, None]
            it = 0
            for jb in range(11):
                i = C["gi"] % 3
                C["gi"] += 1
                t_w = P.dma("pool", wguv[i], wgu_d[:, :, jb * 512:(jb + 1) * 512], d_g[i], after=gfree[i])
                for jj in range(2):
                    j = 2 * jb + jj
                    for tl_ in range(2):
                        lc = slice(tl_ * 512, tl_ * 512 + 512)
                        s = it % 2
                        it += 1
                        bG, bU = PSB[2 * s], PSB[2 * s + 1]
                        tG = tU_ = None
                        for c in range(8):
                            tG = mm(bG[:], wguv[i][:, c, jj * 256:jj * 256 + 128], ftv[:, c, lc], c == 0, c == 7,
                                    after=[t_w, t_ft, gufree[s][0]])
                        for c in range(8):
                            tU_ = mm(bU[:], wguv[i][:, c, jj * 256 + 128:jj * 256 + 256], ftv[:, c, lc], c == 0, c == 7,
                                     after=[gufree[s][1]])
                        t_s = act(sil[s], bG[:], AF.Silu, after=[tG, silfree[s]])
                        gufree[s][0] = t_s
                        t_a = tt(actt[:, j, lc], bU[:], sil[s], ALU.mult, after=[tU_, t_s])
                        gufree[s][1] = t_a
                        silfree[s] = t_a
                gfree[i] = [Tok(P.esem["pe"], P.ecnt["pe"])]
                if jb in hooks:
                    hooks[jb]()

        def c3(pp, hooks, stats=None):
            t_act = Tok(P.esem["dve"], P.ecnt["dve"])
            spend = []
            sq_free = [None] * 8

            def stat_mm():
                t_sq, k, tl_s, dc_s = spend.pop(0)
                sq_free[k] = mm(PSB[tl_s][:], onev, sq2[:, k, :], dc_s == 0, dc_s == 7,
                                after=[t_sq, t_one] + list(stats.get("pre", [])))
                stats["last"][tl_s] = sq_free[k]
            it = 0
            for db in range(4):
                i = C["di"] % 2
                C["di"] += 1
                t_w = P.dma("pool", wdv[i], wd_d[:, :, db * 256:(db + 1) * 256], d_d[i], after=dfree[i])
                for dd in range(2):
                    dc = 2 * db + dd
                    for tl_ in range(2):
                        tg = 2 * pp + tl_
                        cols = slice(tg * 512, tg * 512 + 512)
                        lc = slice(tl_ * 512, tl_ * 512 + 512)
                        b = it % 4
                        it += 1
                        t = None
                        for j in range(NJ):
                            t = mm(PSB[4 + b][:], wdv[i][:, j, dd * 128:dd * 128 + 128], actt[:, j, lc], j == 0, j == NJ - 1,
                                   after=[t_w, t_act, C["ofree2"][b]])
                        C["ofree2"][b] = tt(ht[dc][:, cols], PSB[4 + b][:], ht[dc][:, cols], ALU.add, after=[t])
                        if stats is not None:
                            k = stats["n"] % 8
                            stats["n"] += 1
                            t_sq = act(sq2[:, k, :], ht[dc][:, cols], AF.Square, after=[C["ofree2"][b], sq_free[k]])
                            spend.append((t_sq, k, tl_, dc))
                            if len(spend) > 2:
                                stat_mm()
                dfree[i] = [Tok(P.esem["pe"], P.ecnt["pe"])]
                if db in hooks:
                    hooks[db]()
            while spend:
                stat_mm()

        wo = [bf(TMPB + i * 512, 1024).rearrange("p (c n) -> p c n", c=8) for i in range(8)]
        woutv = wout.rearrange("(c p) n -> p c n", p=128)
        t_wo = [P.dma("pool", wo[dc], woutv[:, :, dc * 128:dc * 128 + 128], P.dsem()) for dc in range(8)]
        steps, st0 = make_c1(0, all_toks, [])
        bfree = [None] * 6
        it = 0
        for tg in range(4):
            cols = slice(tg * 512, tg * 512 + 512)
            for dc in range(8):
                b = it % 6
                it += 1
                t = None
                for c in range(8):
                    t = mm(PSB[2 + b][:], wo[dc][:, c, :], merged[:, c, cols], c == 0, c == 7, after=[t_wo[dc], bfree[b]])
                bfree[b] = tt(ht[dc][:, cols], PSB[2 + b][:], ht[dc][:, cols], ALU.add, after=[t, t_h[dc]])
            if tg < 3:
                steps[tg]()
        t_b3pe = Tok(P.esem["pe"], P.ecnt["pe"])
        for i_ in range(3):
            gfree[i_] = [t_b3pe]
        chk("B3")

        c2(0, Tok(P.esem["dve"], P.ecnt["dve"]), {})
        t_c2pe = Tok(P.esem["pe"], P.ecnt["pe"])
        steps1, st1 = make_c1(1, all_toks(), [t_c2pe])
        steps1[0]()
        c3(0, {0: steps1[1], 1: steps1[2]})
        steps4 = make_c4(0)
        steps4[0]()
        c2(1, st1["t_ft"], {0: steps4[1], 1: steps4[2]})
        t_pre = all_toks()
        fstats = dict(n=0, last=[None, None], pre=t_pre)
        P.wait("act", t_pre)
        c3(1, {}, stats=fstats)
        for tl_ in range(2):
            tg = 2 + tl_
            cols = slice(tg * 512, tg * 512 + 512)
            t_ln = act(rs2[tl_], PSB[tl_][:], AF.Ln, after=[fstats["last"][tl_], t_cst], scale=1.0 / D, bias=cstv[:, 96:97])
            t_ex = act(rs2[tl_], rs2[tl_], AF.Exp, after=[t_ln], scale=-0.5)
            for c in range(8):
                k16 = tl_ * 8 + c
                if k16 < 4:
                    slot, dsem_, pre_ = ost[k16], d_out[k16], [ostfree[k16]]
                else:
                    slot, dsem_, pre_ = f32(WGU + (k16 - 4) * 512, 512), d_out[k16 % 4], []
                t_o = stt(slot, ht[c][:, cols], cstv[:, 16 + c:17 + c], rs2[tl_], ALU.mult, ALU.mult,
                          after=[t_ex] + pre_)
                out_toks.append(P.dma("sp", oT[c * 128:(c + 1) * 128, cols], slot, dsem_, after=[t_o]))
        best = {}
        for t_ in out_toks:
            if id(t_.sem) not in best or best[id(t_.sem)].val < t_.val:
                best[id(t_.sem)] = t_
        P.wait("sp", list(best.values()))


_NC_CACHE = {}


def _host_consts(hf):
    f32 = np.float32
    inv_freq = 500000.0 ** (-(np.arange(0, 16, 2, dtype=np.float64)) / 16.0)
    pos = np.concatenate([np.arange(2048) + (hf - 1) * 2048, np.arange(2048) + hf * 2048]).astype(np.float64)
    ang = pos[None, :] * inv_freq[:, None]
    cs, sn = np.cos(ang).astype(f32), np.sin(ang).astype(f32)
    tab = np.zeros((128, 8192), f32)
    for p in range(128):
        dim = p % 64
        if dim < 8:
            tab[p, 0:4096] = cs[dim]
            tab[p, 4096:] = -sn[dim]
        elif dim < 16:
            tab[p, 0:4096] = cs[dim - 8]
            tab[p, 4096:] = sn[dim - 8]
        else:
            tab[p, 0:4096] = 1.0
    k = np.arange(128)[:, None]
    q = np.arange(128)[None, :]
    mprev = (q <= k).astype(f32)
    mdiag = (q >= k).astype(f32)
    msk = np.concatenate([mprev, mdiag, mprev, mdiag, mprev * hf, mdiag, mprev * hf, mdiag], axis=1).astype(f32)
    prm = np.zeros((128, 128), f32)
    for m in range(128):
        dim = m % 64
        if dim < 8:
            prm[m + 8, m] = 1.0
        elif dim < 16:
            prm[m - 8, m] = 1.0
    return tab, msk, prm


def make_in_maps(x, norm_mix, w_in, w_pool_group, pool_scale, w_att_branch, w_pool_branch,
                 w_out, norm_ffn, w_ffn_gate, w_ffn_up, w_ffn_down, norm_final):
    f32 = np.float32
    x = np.asarray(x, f32)
    w_in0 = np.asarray(w_in, f32)[0]
    cols = []
    for sp in range(4):
        for g in range(3):
            for typ in range(3):
                base = typ * 1536 + g * 512 + sp * 128
                cols.append(np.arange(base, base + 128))
    cols.append(np.arange(4608, 7168))
    cols = np.concatenate(cols)
    w_in_r = np.ascontiguousarray(w_in0[:, cols])
    wg = np.asarray(w_ffn_gate, f32)[0].reshape(D, NJ, 128)
    wu = np.asarray(w_ffn_up, f32)[0].reshape(D, NJ, 128)
    wgu = np.ascontiguousarray(np.stack([wg, wu], axis=2).reshape(D, 2 * DFF))
    wd = np.ascontiguousarray(np.asarray(w_ffn_down, f32)[0])
    wpg = np.ascontiguousarray(np.asarray(w_pool_group, f32)[0].transpose(1, 0, 2).reshape(128, 512))
    wab = np.ascontiguousarray(np.asarray(w_att_branch, f32)[0])
    wpb = np.ascontiguousarray(np.asarray(w_pool_branch, f32)[0])
    wout = np.ascontiguousarray(np.asarray(w_out, f32)[0])

    in_maps = []
    for c in range(8):
        b, hf = c // 2, c % 2
        xT = np.zeros((D, 4096), f32)
        xT[:, 2048:] = x[b, hf * 2048:(hf + 1) * 2048, :].T
        if hf == 1:
            xT[:, :2048] = x[b, 0:2048, :].T
        tab, msk, prm = _host_consts(hf)
        cst = np.zeros((128, 128), f32)
        cst[:, 0:8] = np.asarray(norm_mix, f32)[0].reshape(8, 128).T
        cst[:, 8:16] = np.asarray(norm_ffn, f32)[0].reshape(8, 128).T
        cst[:, 16:24] = np.asarray(norm_final, f32).reshape(8, 128).T
        cst[:, 24:28] = np.asarray(pool_scale, f32)[0].reshape(4, 128).T
        cst[:, 96] = EPS
        for gz in range(4):
            w = 2 ** (gz + 1)
            posj = hf * 2048 + np.arange(16)
            cnt = np.minimum(posj + 1, w).astype(f32)
            cst[:, 28 + gz * 16:28 + gz * 16 + 16] = (f32(1.0) / cnt)[None, :]
        in_maps.append({"xT": xT, "w_in": w_in_r, "tab": tab, "cst": cst, "msk": msk, "prm": prm,
                        "wpg": wpg, "wab": wab, "wpb": wpb, "wout": wout, "wgu": wgu, "wd": wd})
    return in_maps


def kernel(x, norm_mix, w_in, w_pool_group, pool_scale, w_att_branch, w_pool_branch,
           w_out, norm_ffn, w_ffn_gate, w_ffn_up, w_ffn_down, norm_final):
    f32 = np.float32
    in_maps = make_in_maps(x, norm_mix, w_in, w_pool_group, pool_scale, w_att_branch, w_pool_branch,
                           w_out, norm_ffn, w_ffn_gate, w_ffn_up, w_ffn_down, norm_final)
    if "nc" not in _NC_CACHE:
        _NC_CACHE["nc"] = build_program()
    res = run_bass_kernel_spmd(_NC_CACHE["nc"], in_maps, core_ids=list(range(8)))
    out = np.empty((4, S, D), f32)
    for c in range(8):
        b, hf = c // 2, c % 2
        out[b, hf * 2048:(hf + 1) * 2048, :] = res.results[c]["oT"].T
    return out
```

```python
import numpy as np
import concourse.bass as bass
import concourse.mybir as mybir
from concourse.bass_utils import run_bass_kernel_spmd
from contextlib import ExitStack

F32 = mybir.dt.float32
BF16 = mybir.dt.bfloat16
ALU = mybir.AluOpType
AF = mybir.ActivationFunctionType

D = 1024
S = 4096
NT = 2048
DFF = 2816
NJ = 22
EPS = 1e-6
GROUPS = ((128, 1), (512, 4), (2048, 16))
NBLK = (17, 20, 32)
KT_OFF = (0, 2176, 4736)
VS_OFF = (0, 17 * 192, 37 * 192)

NW = 52512
UT_H, UT_O, ATT = 0, 8192, 16384
CST, MSK, PRM, ONE = 20480, 20608, 21120, 21184
R = 21504
TAB, QT, KT, VS, ACC, WP, PT, QRAW, T12 = R, 25600, 28672, 33088, 39712, 43808, 48416, 49440, 49952
XS, SQ, RS = 25600, 33792, 35840
POOLT, MERGED, TMPB = 29696, 33792, 41984
ACTT, RS2 = 8192, 19456
FT, WGU, WD, SQ2, SIL, OST = 29696, 33792, 39936, 46080, 48128, 49152


class Tok:
    __slots__ = ("sem", "val")

    def __init__(self, sem, val):
        self.sem, self.val = sem, val


class DSem:
    def __init__(self, sem):
        self.sem, self.cnt = sem, 0


class Prog:
    ENGS = ("pe", "act", "dve", "pool", "sp")
    CE = ("pe", "act", "dve")

    def __init__(self, nc, es):
        self.nc, self.es = nc, es
        self.ops = {e: [] for e in self.ENGS}
        self.esem = {e: es.enter_context(nc.semaphore("es_" + e)) for e in self.CE}
        self.ecnt = {e: 0 for e in self.CE}
        self.seen = {e: {} for e in self.ENGS}
        self.nsem = 0
        self.dma_out = []

    def dsem(self):
        self.nsem += 1
        return DSem(self.es.enter_context(self.nc.semaphore(f"ds{self.nsem}")))

    def _waits(self, eng, after):
        waits = []
        for t in after:
            if t is None:
                continue
            k = id(t.sem)
            if self.seen[eng].get(k, 0) >= t.val:
                continue
            self.seen[eng][k] = t.val
            waits.append((t.sem, t.val))
        return waits

    def op(self, eng, fn, after=()):
        waits = self._waits(eng, after)
        self.ecnt[eng] += 1
        self.ops[eng].append((waits, fn, (self.esem[eng], 1)))
        return Tok(self.esem[eng], self.ecnt[eng])

    def dma(self, q, out, in_, ds, after=()):
        waits = self._waits(q, after)
        ds.cnt += 16
        self.ops[q].append((waits, lambda e: e.dma_start(out=out, in_=in_), (ds.sem, 16)))
        t = Tok(ds.sem, ds.cnt)
        self.dma_out.append(t)
        return t

    def wait(self, eng, after):
        waits = self._waits(eng, after)
        if waits:
            self.ops[eng].append((waits, None, None))

    def barrier(self):
        best = {}
        for t in self.dma_out:
            if id(t.sem) not in best or best[id(t.sem)].val < t.val:
                best[id(t.sem)] = t
        toks = [Tok(self.esem[e], self.ecnt[e]) for e in self.CE if self.ecnt[e] > 0] + list(best.values())
        self.dma_out = []
        for e in self.ENGS:
            self.wait(e, toks)

    def emit(self):
        with self.nc.Block() as block:
            for e, deco in (("pe", block.tensor), ("act", block.scalar), ("dve", block.vector),
                            ("pool", block.gpsimd), ("sp", block.sync)):
                ops = self.ops[e]

                def body(eng, ops=ops):
                    for waits, fn, inc in ops:
                        for s, v in waits:
                            eng.wait_ge(s, v)
                        if fn is not None:
                            ins = fn(eng)
                            if inc is not None:
                                ins.then_inc(inc[0], inc[1])

                deco(body)


def cm_view(ap, g, m):
    if g == 0:
        return ap[:, 512 * m:512 * m + 512]
    if g == 1:
        return ap[:, m:2048:4]
    return ap.rearrange("p (i r) -> p r i", r=16)[:, 4 * m:4 * m + 4, :]


def cont_view(ap, g):
    if g == 2:
        return ap.rearrange("p (r i) -> p r i", r=4)
    return ap


class _Stop(Exception):
    pass


DEBUG = {"stop": None, "dump": (0, 16)}


def build_program():
    nc = bass.Bass("TRN2", target_bir_lowering=False)
    dr = lambda n, s, k="ExternalInput": nc.dram_tensor(n, list(s), F32, kind=k).ap()
    xT = dr("xT", [D, 4096])
    w_in = dr("w_in", [D, 7168])
    tab = dr("tab", [128, 8192])
    cst = dr("cst", [128, 128])
    msk = dr("msk", [128, 1024])
    prm = dr("prm", [128, 128])
    wpg = dr("wpg", [128, 512])
    wab = dr("wab", [512, D])
    wpb = dr("wpb", [512, D])
    wout = dr("wout", [D, D])
    wgu = dr("wgu", [D, 2 * DFF])
    wd = dr("wd", [DFF, D])
    oT = dr("oT", [D, NT], "ExternalOutput")
    if DEBUG["stop"] is not None:
        dbg = dr("dbg", [128, DEBUG["dump"][1]], "ExternalOutput")
    else:
        dbg = None

    with ExitStack() as es:
        P = Prog(nc, es)
        try:
            _body(nc, es, P, locals())
        except _Stop:
            pass
        P.emit()
    return nc


def _body(nc, es, P, L):
    xT, w_in, tab, cst, msk, prm, wpg, wab, wpb, wout, wgu, wd, oT = [
        L[k] for k in ("xT", "w_in", "tab", "cst", "msk", "prm", "wpg", "wab", "wpb", "wout", "wgu", "wd", "oT")]
    dbg = L["dbg"]

    def chk(name):
        if DEBUG["stop"] == name:
            P.barrier()
            o, n = DEBUG["dump"]
            t = P.dma("sp", dbg, AR[:, o:o + n], P.dsem())
            P.wait("sp", [t])
            raise _Stop()
    if True:
        AR = es.enter_context(nc.sbuf_tensor("arena", [128, NW], F32))
        PS_ALL = es.enter_context(nc.psum_tensor("ps_all", [128, 4096], F32))
        PSB = [PS_ALL[:, i * 512:(i + 1) * 512] for i in range(8)]

        def f32(off, n):
            return AR[:, off:off + n]

        def bf(off, n):
            return AR[:, off:off + n // 2].bitcast(BF16)

        def mm(out, lhsT, rhs, start, stop, after=()):
            return P.op("pe", lambda e: e.matmul(out, lhsT=lhsT, rhs=rhs, start=start, stop=stop), after)

        def act(out, in_, func, after=(), **kw):
            return P.op("act", lambda e: e.activation(out=out, in_=in_, func=func, **kw), after)

        def tt(out, in0, in1, op, after=()):
            return P.op("dve", lambda e: e.tensor_tensor(out=out, in0=in0, in1=in1, op=op), after)

        def stt(out, in0, scalar, in1, op0, op1, after=()):
            return P.op("dve", lambda e: e.scalar_tensor_tensor(out=out, in0=in0, scalar=scalar, in1=in1,
                                                                op0=op0, op1=op1), after)

        uth = bf(UT_H, 8 * 2048).rearrange("p (c t) -> p c t", c=8)
        uto = bf(UT_O, 8 * 2048).rearrange("p (c t) -> p c t", c=8)
        att = bf(ATT, 4 * 2048).rearrange("p (c t) -> p c t", c=4)
        cstv = f32(CST, 128)
        mskv = bf(MSK, 1024)
        prmv = bf(PRM, 128)
        onev = bf(ONE, 128)
        tabv = bf(TAB, 8192)
        tcos_h, tcos_o = tabv[:, 0:2048], tabv[:, 2048:4096]
        tsin_h, tsin_o = tabv[:, 4096:6144], tabv[:, 6144:8192]

        t_cst = P.dma("sp", cstv, cst, P.dsem())
        t_msk = P.dma("pool", mskv, msk, P.dsem())
        t_prm = P.dma("pool", prmv, prm, P.dsem())
        t_ms1 = P.op("dve", lambda e: e.memset(tabv[:, 0:4096], 1.0))
        t_ms2 = P.op("dve", lambda e: e.memset(tabv[:, 4096:8192], 0.0))
        d_tab = P.dsem()
        for p0 in (0, 64):
            t_tab = P.dma("pool", tabv[p0:p0 + 16, :].rearrange("p (a b) -> p a b", b=1024),
                          tab[p0:p0 + 16, :].rearrange("p (a b) -> p a b", b=1024), d_tab, after=[t_ms1, t_ms2])
        t_one = P.op("dve", lambda e: e.memset(onev, 1.0))
        t_const = [t_cst, t_msk, t_prm, t_tab, t_one]

        wpv = bf(WP, 8 * 1152).rearrange("p (c n) -> p c n", c=8)
        w_inv = w_in.rearrange("(c p) n -> p c n", p=128)
        d_wp = P.dsem()
        t_wp0 = P.dma("pool", wpv, w_inv[:, :, 0:1152], d_wp)

        def rms_stats(src_fn, sqv, rsv, psb, after, rs_after=()):
            ts = []
            after = list(after) + [Tok(P.esem["pe"], P.ecnt["pe"])]
            for c in range(8):
                ts.append(act(sqv[:, c, :], src_fn(c), AF.Square, after=after))
            t = None
            for c in range(8):
                t = mm(psb[:], onev, sqv[:, c, :], c == 0, c == 7, after=[ts[c], t_one] + list(after))
            t = act(rsv, psb[:], AF.Ln, after=[t, t_cst] + list(rs_after), scale=1.0 / D, bias=cstv[:, 96:97])
            t = act(rsv, rsv, AF.Exp, after=[t], scale=-0.5)
            return t

        xTv = xT.rearrange("(c p) t -> p c t", p=128)
        xs = [f32(XS + i * 4096, 4096).rearrange("p (c t) -> p c t", c=8) for i in range(3)]
        sqv2 = [bf(37888 + j * 2048, 4096).rearrange("p (c t) -> p c t", c=8) for j in range(2)]
        rsv2 = [f32(41984, 512), f32(42496, 512)]
        d_x = [P.dsem(), P.dsem(), P.dsem()]
        xs_free = [[], [], []]
        rs_free = [[], []]
        for tg in range(8):
            i = tg % 3
            j = tg % 2
            t_ld = P.dma("sp", xs[i], xTv[:, :, tg * 512:(tg + 1) * 512], d_x[i], after=xs_free[i])
            t_rs = rms_stats(lambda c: xs[i][:, c, :], sqv2[j], rsv2[j], PSB[tg % 2], [t_ld], rs_after=rs_free[j])
            dst = uth if tg < 4 else uto
            col = (tg % 4) * 512
            tl = []
            for c in range(8):
                tl.append(stt(dst[:, c, col:col + 512], xs[i][:, c, :], cstv[:, c:c + 1], rsv2[j],
                              ALU.mult, ALU.mult, after=[t_rs, t_cst]))
            xs_free[i] = [tl[-1]]
            rs_free[j] = [tl[-1]]
        P.barrier()
        chk("P0")

        qtv = [bf(QT + g * 1024, 2048) for g in range(3)]
        ktv = [bf(KT, 8832)[:, KT_OFF[g]:KT_OFF[g] + NBLK[g] * 128] for g in range(3)]
        vsv = [bf(VS, 69 * 192)[:, VS_OFF[g]:VS_OFF[g] + NBLK[g] * 192].rearrange("p (b k) -> p b k", k=192)
               for g in range(3)]
        accv = [f32(ACC + e * 2048, 2048) for e in range(2)]
        ptv = [bf(PT + i * 256, 512) for i in range(4)] + [bf(52000 + i * 256, 512) for i in range(2)]
        qrawv = [bf(QRAW + i * 256, 512) for i in range(2)]
        t1v = [f32(T12 + i * 512, 512) for i in range(2)]
        t2v = [f32(T12 + 1024 + i * 512, 512) for i in range(2)]
        ntmp = f32(T12, 2048)

        vs_all = bf(VS, 69 * 192).rearrange("p (b k) -> p b k", k=192)
        P.op("dve", lambda e: e.memset(vs_all[:, :, 64:128], 1.0))
        t_wp = t_wp0

        XS_ = dict(psq=[None, None], pss=[None, None], v=[None] * 4, t1=[None, None], tpe=None)
        for sp in range(4):
            units = []

            def cmdst(region, d, m):
                if d == 1:
                    return region[:, 512 * m:512 * m + 512]
                w = 512 // d
                return region.rearrange("p (r i) -> p r i", r=d)[:, :, w * m:w * m + w]

            for g in range(3):
                d = GROUPS[g][1]
                for typ in (0, 1):
                    wcol = (g * 3 + typ) * 128
                    dstb = qtv[g] if typ == 0 else ktv[g][:, 0:2048]
                    for m in range(4):
                        units.append((cmdst(dstb, d, m), wcol,
                                      (lambda c, m=m: uto[:, c, 512 * m:512 * m + 512]),
                                      tcos_o[:, 512 * m:512 * m + 512], tsin_o[:, 512 * m:512 * m + 512], 512, d))
                wcol = (g * 3 + 1) * 128
                if g == 2:
                    for m in range(4):
                        units.append((cmdst(ktv[g][:, 2048:4096], 16, m), wcol,
                                      (lambda c, m=m: uth[:, c, 512 * m:512 * m + 512]),
                                      tcos_h[:, 512 * m:512 * m + 512], tsin_h[:, 512 * m:512 * m + 512], 512, 16))
                elif g == 1:
                    units.append((ktv[g][:, 2048:2560].rearrange("p (r i) -> p r i", r=4), wcol,
                                  (lambda c: uth[:, c, 1536:2048]), tcos_h[:, 1536:2048], tsin_h[:, 1536:2048], 512, 4))
                else:
                    units.append((ktv[g][:, 2048:2176], wcol, (lambda c: uth[:, c, 1920:2048]),
                                  tcos_h[:, 1920:2048], tsin_h[:, 1920:2048], 128, 1))

            P.wait("pe", [t_wp])
            chk(f"A{sp}Pw")
            if DEBUG.get("units") is not None:
                units = [units[k] for k in DEBUG["units"]]
            v_batches = []
            for g in range(3):
                d = GROUPS[g][1]
                blocks = []
                for b in range(16):
                    r, qb = divmod(b, 16 // d)
                    blocks.append((b, False, r + d * 128 * qb))
                for r in range(d):
                    blocks.append((16 + r, True, 2048 - 128 * d + r))
                for b0 in range(0, len(blocks), 4):
                    v_batches.append((g, blocks[b0:b0 + 4]))
            vb_i = [0]
            vfree = list(XS_["v"])

            def emit_vbatch():
                vbank = vb_i[0]
                g, batch = v_batches[vbank]
                vb_i[0] += 1
                d = GROUPS[g][1]
                wcol = (g * 3 + 2) * 128
                nb = len(batch)
                psv = PSB[4 + vbank % 4]
                t = None
                for k, (b, hist, s0) in enumerate(batch):
                    src = uth if hist else uto
                    for c in range(8):
                        t = mm(psv[:, k * 128:(k + 1) * 128], src[:, c, s0:s0 + 127 * d + 1:d],
                               wpv[:, c, wcol:wcol + 128], c == 0, c == 7, after=[t_wp, vfree[vbank % 4]])
                bfirst = batch[0][0]
                outv = vsv[g][:, bfirst:bfirst + nb, :].rearrange("p b (h k) -> p b h k", k=64)[:, :, 0:3:2, :]
                inv = psv[:, 0:nb * 128].rearrange("p (b h k) -> p b h k", b=nb, h=2)
                vfree[vbank % 4] = act(outv, inv, AF.Copy, after=[t, XS_["tpe"]])

            psq_free = list(XS_["psq"])
            pss_free = list(XS_["pss"])
            qraw_free = [None, None]
            t1_free = list(XS_["t1"])
            pend = None

            def do_perm(pd):
                (ui, t_raw, i, N, shp, dst, cosv, sinv, t_t1) = pd
                pss = PSB[2 + i]
                LV = DEBUG.get("lv", 9)
                if LV < 3:
                    return
                t_p = mm(pss[:, 0:N], prmv, qrawv[i][:, 0:N], True, True, after=[t_raw, pss_free[i], t_prm])
                if LV < 4:
                    return
                t_t2 = tt(t2v[i][:, 0:N], pss[:, 0:N], sinv, ALU.mult, after=[t_p, t_tab, t1_free[i]])
                if LV < 5:
                    return
                t_o = tt(dst, shp(t1v[i][:, 0:N]), shp(t2v[i][:, 0:N]), ALU.add, after=[t_t1, t_t2, XS_["tpe"]])
                pss_free[i] = t_t2
                qraw_free[i] = t_p
                t1_free[i] = t_o

            for ui, (dst, wcol, rhs_fn, cosv, sinv, N, dd) in enumerate(units):
                i = ui % 2
                psq = PSB[i]
                shp = (lambda a, dd=dd: a.rearrange("p (i r) -> p r i", r=dd)) if dd > 1 else (lambda a: a)
                t = None
                for c in range(8):
                    t = mm(psq[:, 0:N], wpv[:, c, wcol:wcol + 128], rhs_fn(c), c == 0, c == 7,
                           after=[t_wp, psq_free[i]])
                t_raw = act(qrawv[i][:, 0:N], psq[:, 0:N], AF.Copy, after=[t, qraw_free[i]])
                t_t1 = tt(t1v[i][:, 0:N], psq[:, 0:N], cosv, ALU.mult, after=[t, t_raw, t1_free[i], t_tab])
                psq_free[i] = t_t1
                if pend is not None:
                    do_perm(pend)
                pend = (ui, t_raw, i, N, shp, dst, cosv, sinv, t_t1)
                while vb_i[0] < len(v_batches) and vb_i[0] * len(units) < (ui + 1) * len(v_batches):
                    emit_vbatch()
            do_perm(pend)
            chk(f"A{sp}Pqk")

            while vb_i[0] < len(v_batches):
                emit_vbatch()
            t_Pdone = [Tok(P.esem["dve"], P.ecnt["dve"]), Tok(P.esem["act"], P.ecnt["act"])]
            t_Ppe = Tok(P.esem["pe"], P.ecnt["pe"])
            chk(f"A{sp}P")
            if sp < 3:
                t_wp = P.dma("pool", wpv, w_inv[:, :, (sp + 1) * 1152:(sp + 2) * 1152], d_wp, after=[t_Ppe])
            else:
                zw = bf(WP, 4096).rearrange("p (c n) -> p c n", c=8)
                wpgv = bf(WP + 2048, 512)
                d_b = P.dsem()
                P.dma("pool", zw, w_inv[:, :, 4608:5120], d_b, after=[t_Ppe])
                t_wpg = P.dma("pool", wpgv, wpg, d_b)
                t_zw = t_wpg

            tiles = []
            nbatch = 0
            for m in range(4):
                lst = []
                for k in range(4):
                    b = 4 * m + k
                    lst.append((0, b, (b - 1) if b >= 1 else 16, b >= 1, slice(k * 128, k * 128 + 128)))
                for r4 in range(4):
                    b = r4 * 4 + m
                    lst.append((1, b, (b - 1) if m >= 1 else 16 + r4, m >= 1, slice(r4, 512, 4)))
                for j, (g_, b_, pb_, nm_, oc_) in enumerate(lst):
                    tiles.append((g_, b_, pb_, nm_, nbatch % 2, oc_, j == 0, j == len(lst) - 1, ("copy", m)))
                nbatch += 1
            for m in range(4):
                for k in range(4):
                    b = 4 * m + k
                    tiles.append((2, b, 16 + b, False, nbatch % 2, slice(k * 128, k * 128 + 128), k == 0, k == 3,
                                  ("add2", m)))
                nbatch += 1
            sfree = [[psq_free[0], psq_free[1]], [pss_free[0], pss_free[1]]]
            ptfree = [None] * 6
            ofree = [[vfree[0], vfree[1]], [vfree[2], vfree[3]]]
            pend_pv = []
            acc_last = [None, None]

            def do_pv(item):
                (ti, tile, t_m) = item
                (g, b, prevb, normal, par, oc, first, last, evac) = tile
                i4 = ti % 6
                tl = None
                for e in range(2):
                    pso = PSB[4 + 2 * e + par]
                    o = pso[:, oc]
                    P.op("pe", lambda en, o=o, l=vsv[g][:, prevb, e * 64:e * 64 + 128], r=ptv[i4][:, e * 256:e * 256 + 128],
                         st=first: en.matmul(o, lhsT=l, rhs=r, start=st, stop=False, skip_group_check=True),
                         [t_m, ofree[e][par]])
                    tl = P.op("pe", lambda en, o=o, l=vsv[g][:, b, e * 64:e * 64 + 128],
                              r=ptv[i4][:, e * 256 + 128:e * 256 + 256], sp_=last:
                              en.matmul(o, lhsT=l, rhs=r, start=False, stop=sp_, skip_group_check=True))
                ptfree[i4] = tl
                if last:
                    kind, m = evac
                    for e in range(2):
                        pso = PSB[4 + 2 * e + par]
                        if kind == "copy":
                            ofree[e][par] = act(accv[e][:, 512 * m:512 * m + 512], pso[:], AF.Copy,
                                                after=[tl, XS_["t1"][0]])
                        else:
                            av = cm_view(accv[e], 2, m)
                            ofree[e][par] = tt(av, cont_view(pso[:], 2), av, ALU.add, after=[tl, acc_last[e]])
                        acc_last[e] = ofree[e][par]

            for ti, tile in enumerate(tiles):
                (g, b, prevb, normal) = tile[0:4]
                i4 = ti % 6
                i2 = ti % 2
                t = None
                for kb_, c0_ in ((prevb, 0), (b, 128)):
                    for e in range(2):
                        ps = PSB[2 * i2 + e]
                        pr = slice(64 * e, 64 * e + 64)
                        q = qtv[g][pr, b * 128:(b + 1) * 128]
                        t = mm(ps[:, c0_:c0_ + 128], ktv[g][pr, kb_ * 128:(kb_ + 1) * 128], q, True, True,
                               after=list(sfree[i2]) + t_Pdone)
                ps2 = PS_ALL[:, 2 * i2 * 512:(2 * i2 + 2) * 512].rearrange("p (e c) -> p e c", e=2)[:, :, 0:256]
                t_e = act(ptv[i4].rearrange("p (e c) -> p e c", e=2), ps2, AF.Exp, after=[t, ptfree[i4]], scale=0.125)
                sfree[i2] = [t_e]
                mv = mskv[:, 0:512] if normal else mskv[:, 512:1024]
                t_m = tt(ptv[i4], ptv[i4], mv, ALU.mult, after=[t_e, t_msk])
                pend_pv.append((ti, tile, t_m))
                if len(pend_pv) > 4:
                    do_pv(pend_pv.pop(0))
            while pend_pv:
                do_pv(pend_pv.pop(0))
            tU = Tok(P.esem["dve"], P.ecnt["dve"])
            tA = Tok(P.esem["act"], P.ecnt["act"])
            t_c0 = act(ntmp[0:64, :], accv[0][64:128, :], AF.Ln, after=[tU, tA])
            t_c1 = act(ntmp[64:128, :], accv[1][0:64, :], AF.Ln, after=[tU, tA])
            t_l = act(ntmp, ntmp, AF.Exp, after=[t_c0, t_c1], scale=-1.0)
            tt(att[0:64, sp, :], accv[0][0:64, :], ntmp[0:64, :], ALU.mult, after=[t_l])
            t_nrm = tt(att[64:128, sp, :], accv[1][64:128, :], ntmp[64:128, :], ALU.mult, after=[t_l])
            if sp == 3:
                P.barrier()
            else:
                XS_["psq"] = [sfree[0][0], sfree[0][0]]
                XS_["pss"] = [sfree[1][0], sfree[1][0]]
                XS_["v"] = [ofree[0][0], ofree[0][1], ofree[1][0], ofree[1][1]]
                XS_["t1"] = [t_nrm, t_nrm]
                XS_["tpe"] = Tok(P.esem["pe"], P.ecnt["pe"])
            chk(f"A{sp}")

        ht = [f32(UT_H + c * 2048, 2048) if c < 4 else f32(R + (c - 4) * 2048, 2048) for c in range(8)]
        d_h = P.dsem()
        t_h = [None] * 8
        for c in range(4, 8):
            t_h[c] = P.dma("sp", ht[c], xT[c * 128:(c + 1) * 128, 2048:4096], d_h)
        poolt = bf(POOLT, 4 * 2048).rearrange("p (c t) -> p c t", c=4)
        merged = bf(MERGED, 8 * 2048).rearrange("p (c t) -> p c t", c=8)
        pre = [bf(TMPB + i * 256, 512) for i in range(4)]
        o = WP + 2048 + 256
        zbuf = [[f32(o + (2 * i + k) * 528, 528) for k in range(2)] for i in range(4)]; o += 4224
        ta, tb = f32(o, 528), f32(o + 528, 528); o += 1056
        t16 = f32(o, 16); o += 16
        assert o <= NW
        halo_tok = [None] * 4
        psz_free = [None] * 4
        for gz in range(4):
            t = None
            for c in range(8):
                t = mm(PSB[gz][:, 0:16], zw[:, c, gz * 128:(gz + 1) * 128], uth[:, c, 2032:2048], c == 0, c == 7,
                       after=[t_zw])
            halo_tok[gz] = act(zbuf[gz][0][:, 0:16], PSB[gz][:, 0:16], AF.Copy, after=[t])
            psz_free[gz] = halo_tok[gz]
        t_halo = Tok(P.esem["pe"], P.ecnt["pe"])
        for c in range(4):
            t_h[c] = P.dma("sp", ht[c], xT[c * 128:(c + 1) * 128, 2048:4096], d_h, after=[t_halo])
        t_h = [t_h[3]] * 8
        pre_free = [None] * 4
        ppfree = [None] * 4
        zb_free = [[None, None] for _ in range(4)]
        b1_pend = []
        tab_last = [None]

        def b1_pool(item):
            gz, cols, t_p = item
            psp = PSB[4 + gz]
            t_m = mm(psp[:], wpgv[:, gz * 128:(gz + 1) * 128], pre[gz], True, True,
                     after=[t_p, t_wpg, ppfree[gz]])
            pre_free[gz] = t_m
            ppfree[gz] = act(poolt[:, gz, cols], psp[:], AF.Copy, after=[t_m], scale=cstv[:, 24 + gz:25 + gz])
        for tg in range(4):
            cols = slice(tg * 512, tg * 512 + 512)
            kb = tg % 2
            for gz in range(4):
                psz = PSB[gz]
                z = zbuf[gz][kb]
                zn = zbuf[gz][1 - kb]
                t = None
                for c in range(8):
                    t = mm(psz[:], zw[:, c, gz * 128:(gz + 1) * 128], uto[:, c, cols], c == 0, c == 7,
                           after=[t_zw, psz_free[gz]])
                t_z = act(z[:, 16:528], psz[:], AF.Copy, after=[t, zb_free[gz][kb]])
                psz_free[gz] = t_z
                t_hn = act(zn[:, 0:16], z[:, 512:528], AF.Copy, after=[t_z, zb_free[gz][1 - kb]])
                w = 2 ** (gz + 1)
                cur, lo, t_c = z, 0, t_z
                bufs = [ta, tb]
                sh = 1
                for lvl in range(gz + 1):
                    nxt = bufs[lvl % 2]
                    nlo = lo + sh
                    t_c = tt(nxt[:, nlo:528], cur[:, nlo:528], cur[:, nlo - sh:528 - sh], ALU.add,
                             after=[t_c, halo_tok[gz], pre_free[gz], tab_last[0]] if lvl == 0 else [t_c])
                    cur, lo = nxt, nlo
                    sh *= 2
                t_p = stt(pre[gz], cur[:, 16:528], 1.0 / w, z[:, 16:528], ALU.mult, ALU.subtract,
                          after=[t_c, pre_free[gz]])
                if tg == 0:
                    t_f = tt(t16, cur[:, 16:32], cstv[:, 28 + gz * 16:28 + gz * 16 + 16], ALU.mult, after=[t_p])
                    t_p = tt(pre[gz][:, 0:16], t16, z[:, 16:32], ALU.subtract, after=[t_f])
                zb_free[gz][kb] = t_p
                tab_last[0] = t_p
                halo_tok[gz] = t_hn
                b1_pend.append((gz, cols, t_p))
                if len(b1_pend) > 2:
                    b1_pool(b1_pend.pop(0))
        while b1_pend:
            b1_pool(b1_pend.pop(0))
        P.barrier()
        chk("B1")
        o = TMPB
        ring = []
        for i in range(2):
            ga = bf(o, 1024).rearrange("p (c n) -> p c n", c=8); o += 512
            gp = bf(o, 1024).rearrange("p (c n) -> p c n", c=8); o += 512
            ab = bf(o, 512).rearrange("p (c n) -> p c n", c=4); o += 256
            pb = bf(o, 512).rearrange("p (c n) -> p c n", c=4); o += 256
            ring.append((ga, gp, ab, pb))
        sig = [[f32(o + (2 * i + k) * 512, 512) for k in range(2)] for i in range(2)]; o += 2048
        m12 = [[f32(o + (2 * i + k) * 512, 512) for k in range(2)] for i in range(2)]; o += 2048
        wabv = wab.rearrange("(c p) n -> p c n", p=128)
        wpbv = wpb.rearrange("(c p) n -> p c n", p=128)
        d_r = [P.dsem(), P.dsem()]
        rfree = [[], []]
        sgfree = [None, None]
        mfree = [None, None]
        pfree = [[None] * 4, [None] * 4]
        it = 0
        for dc in range(8):
            i = dc % 2
            ga, gp, ab, pb = ring[i]
            cs = slice(dc * 128, dc * 128 + 128)
            P.dma("pool", ga, w_inv[:, :, 5120 + dc * 128:5120 + dc * 128 + 128], d_r[i], after=rfree[i])
            P.dma("pool", gp, w_inv[:, :, 6144 + dc * 128:6144 + dc * 128 + 128], d_r[i])
            P.dma("pool", ab, wabv[:, :, cs], d_r[i])
            t_w = P.dma("pool", pb, wpbv[:, :, cs], d_r[i])
            for tg in range(4):
                cols = slice(tg * 512, tg * 512 + 512)
                s = it % 2
                it += 1
                bA, bP, bYA, bYP = [PSB[4 * s + k] for k in range(4)]
                tA = tP = tYA = tYP = None
                for c in range(8):
                    tA = mm(bA[:], ga[:, c, :], uto[:, c, cols], c == 0, c == 7, after=[t_w, pfree[s][0]])
                for c in range(8):
                    tP = mm(bP[:], gp[:, c, :], uto[:, c, cols], c == 0, c == 7, after=[pfree[s][1]])
                for c in range(4):
                    tYA = mm(bYA[:], ab[:, c, :], att[:, c, cols], c == 0, c == 3, after=[pfree[s][2]])
                for c in range(4):
                    tYP = mm(bYP[:], pb[:, c, :], poolt[:, c, cols], c == 0, c == 3, after=[pfree[s][3]])
                t_sa = act(sig[s][0], bA[:], AF.Sigmoid, after=[tA, sgfree[s]])
                t_sp = act(sig[s][1], bP[:], AF.Sigmoid, after=[tP, sgfree[s]])
                pfree[s][0], pfree[s][1] = t_sa, t_sp
                t_m1 = tt(m12[s][0], bYA[:], sig[s][0], ALU.mult, after=[tYA, t_sa, mfree[s]])
                t_m2 = tt(m12[s][1], bYP[:], sig[s][1], ALU.mult, after=[tYP, t_sp, mfree[s]])
                pfree[s][2], pfree[s][3] = t_m1, t_m2
                sgfree[s] = t_m2
                t_mg = tt(merged[:, dc, cols], m12[s][0], m12[s][1], ALU.add, after=[t_m1, t_m2])
                mfree[s] = t_mg
            rfree[i] = [Tok(P.esem["pe"], P.ecnt["pe"])]
        P.barrier()
        chk("B2")
        actt = bf(ACTT, NJ * 1024).rearrange("p (j t) -> p j t", j=NJ)
        rs2 = [f32(RS2, 512), f32(RS2 + 512, 512)]
        ftv = bf(FT, 8 * 1024).rearrange("p (c t) -> p c t", c=8)
        wguv = [bf(WGU + i * 2048, 4096).rearrange("p (c n) -> p c n", c=8) for i in range(3)]
        wdv = [bf(WD + i * 2816, 5632).rearrange("p (j n) -> p j n", j=NJ) for i in range(2)]
        sq2 = bf(SQ2, 4096).rearrange("p (c t) -> p c t", c=8)
        sil = [f32(SIL + i * 512, 512) for i in range(2)]
        ost = [f32(OST + i * 512, 512) for i in range(4)]
        wgu_d = wgu.rearrange("(c p) n -> p c n", p=128)
        wd_d = wd.rearrange("(j p) n -> p j n", p=128)
        d_g = [P.dsem() for _ in range(3)]
        d_d = [P.dsem() for _ in range(2)]
        d_out = [P.dsem() for _ in range(4)]
        gfree = [[], [], []]
        dfree = [[], []]
        ostfree = [None] * 4
        out_toks = []
        C = dict(gi=0, di=0, oi=0, ofree2=[None] * 4)

        def norm_a(src_fn, after):
            return [act(sq2[:, c, :], src_fn(c), AF.Square, after=after) for c in range(8)]

        def norm_b(ts, psb, rsv):
            t = None
            for c in range(8):
                t = mm(psb[:], onev, sq2[:, c, :], c == 0, c == 7, after=[ts[c], t_one])
            t_ln = act(rsv, psb[:], AF.Ln, after=[t, t_cst], scale=1.0 / D, bias=cstv[:, 96:97])
            t_ex = act(rsv, rsv, AF.Exp, after=[t_ln], scale=-0.5)
            return t, t_ln, t_ex

        def all_toks():
            return [Tok(P.esem[e], P.ecnt[e]) for e in ("pe", "act", "dve")]

        def make_c1(pp, first_after, ft_after):
            st = {}

            def src(tl_):
                tg = 2 * pp + tl_
                return lambda c: ht[c][:, tg * 512:tg * 512 + 512]

            def fin(tl_, t_ex):
                tg = 2 * pp + tl_
                for c in range(8):
                    st["t_ft"] = stt(ftv[:, c, tl_ * 512:tl_ * 512 + 512], ht[c][:, tg * 512:tg * 512 + 512],
                                     cstv[:, 8 + c:9 + c], rs2[tl_], ALU.mult, ALU.mult, after=[t_ex] + list(ft_after))

            def s0():
                st["ts"] = norm_a(src(0), first_after() if callable(first_after) else first_after)

            def s1():
                t_mm, t_ln, t_ex = norm_b(st["ts"], PSB[0], rs2[0])
                fin(0, t_ex)
                st["ts"] = norm_a(src(1), [t_mm, Tok(P.esem["dve"], P.ecnt["dve"])])

            def s2():
                t_mm, t_ln, t_ex = norm_b(st["ts"], PSB[1], rs2[1])
                fin(1, t_ex)
            return [s0, s1, s2], st

        def make_c4(pp):
            st = {}

            def src(tl_):
                tg = 2 * pp + tl_
                return lambda c: ht[c][:, tg * 512:tg * 512 + 512]

            def fin(tl_, t_ex):
                tg = 2 * pp + tl_
                cols = slice(tg * 512, tg * 512 + 512)
                for c in range(8):
                    k = C["oi"] % 4
                    C["oi"] += 1
                    t_o = stt(ost[k], ht[c][:, cols], cstv[:, 16 + c:17 + c], rs2[tl_], ALU.mult, ALU.mult,
                              after=[t_ex, ostfree[k]])
                    ostfree[k] = P.dma("sp", oT[c * 128:(c + 1) * 128, cols], ost[k], d_out[k], after=[t_o])
                    out_toks.append(ostfree[k])

            def s0():
                st["ts"] = norm_a(src(0), all_toks())

            def s1():
                t_mm, t_ln, t_ex = norm_b(st["ts"], PSB[4], rs2[0])
                C["ofree2"][0] = t_ln
                fin(0, t_ex)
                st["ts"] = norm_a(src(1), [t_mm])

            def s2():
                t_mm, t_ln, t_ex = norm_b(st["ts"], PSB[5], rs2[1])
                C["ofree2"][1] = t_ln
                fin(1, t_ex)
            return [s0, s1, s2]

        def c2(pp, t_ft, hooks):
            gufree = [[None, None], [None, None]]
            silfree = [None, None]
            it = 0
            for jb in range(11):
                i = C["gi"] % 3
                C["gi"] += 1
                t_w = P.dma("pool", wguv[i], wgu_d[:, :, jb * 512:(jb + 1) * 512], d_g[i], after=gfree[i])
                for jj in range(2):
                    j = 2 * jb + jj
                    for tl_ in range(2):
                        lc = slice(tl_ * 512, tl_ * 512 + 512)
                        s = it % 2
                        it += 1
                        bG, bU = PSB[2 * s], PSB[2 * s + 1]
                        tG = tU_ = None
                        for c in range(8):
                            tG = mm(bG[:], wguv[i][:, c, jj * 256:jj * 256 + 128], ftv[:, c, lc], c == 0, c == 7,
                                    after=[t_w, t_ft, gufree[s][0]])
                        for c in range(8):
                            tU_ = mm(bU[:], wguv[i][:, c, jj * 256 + 128:jj * 256 + 256], ftv[:, c, lc], c == 0, c == 7,
                                     after=[gufree[s][1]])
                        t_s = act(sil[s], bG[:], AF.Silu, after=[tG, silfree[s]])
                        gufree[s][0] = t_s
                        t_a = tt(actt[:, j, lc], bU[:], sil[s], ALU.mult, after=[tU_, t_s])
                        gufree[s][1] = t_a
                        silfree[s] = t_a
                gfree[i] = [Tok(P.esem["pe"], P.ecnt["pe"])]
                if jb in hooks:
                    hooks[jb]()

        def c3(pp, hooks, stats=None):
            t_act = Tok(P.esem["dve"], P.ecnt["dve"])
            spend = []
            sq_free = [None] * 8

            def stat_mm():
                t_sq, k, tl_s, dc_s = spend.pop(0)
                sq_free[k] = mm(PSB[tl_s][:], onev, sq2[:, k, :], dc_s == 0, dc_s == 7,
                                after=[t_sq, t_one] + list(stats.get("pre", [])))
                stats["last"][tl_s] = sq_free[k]
            it = 0
            for db in range(4):
                i = C["di"] % 2
                C["di"] += 1
                t_w = P.dma("pool", wdv[i], wd_d[:, :, db * 256:(db + 1) * 256], d_d[i], after=dfree[i])
                for dd in range(2):
                    dc = 2 * db + dd
                    for tl_ in range(2):
                        tg = 2 * pp + tl_
                        cols = slice(tg * 512, tg * 512 + 512)
                        lc = slice(tl_ * 512, tl_ * 512 + 512)
                        b = it % 4
                        it += 1
                        t = None
                        for j in range(NJ):
                            t = mm(PSB[4 + b][:], wdv[i][:, j, dd * 128:dd * 128 + 128], actt[:, j, lc], j == 0, j == NJ - 1,
                                   after=[t_w, t_act, C["ofree2"][b]])
                        C["ofree2"][b] = tt(ht[dc][:, cols], PSB[4 + b][:], ht[dc][:, cols], ALU.add, after=[t])
                        if stats is not None:
                            k = stats["n"] % 8
                            stats["n"] += 1
                            t_sq = act(sq2[:, k, :], ht[dc][:, cols], AF.Square, after=[C["ofree2"][b], sq_free[k]])
                            spend.append((t_sq, k, tl_, dc))
                            if len(spend) > 2:
                                stat_mm()
                dfree[i] = [Tok(P.esem["pe"], P.ecnt["pe"])]
                if db in hooks:
                    hooks[db]()
            while spend:
                stat_mm()

        wo = [bf(TMPB + i * 512, 1024).rearrange("p (c n) -> p c n", c=8) for i in range(8)]
        woutv = wout.rearrange("(c p) n -> p c n", p=128)
        t_wo = [P.dma("pool", wo[dc], woutv[:, :, dc * 128:dc * 128 + 128], P.dsem()) for dc in range(8)]
        steps, st0 = make_c1(0, all_toks, [])
        bfree = [None] * 6
        it = 0
        for tg in range(4):
            cols = slice(tg * 512, tg * 512 + 512)
            for dc in range(8):
                b = it % 6
                it += 1
                t = None
                for c in range(8):
                    t = mm(PSB[2 + b][:], wo[dc][:, c, :], merged[:, c, cols], c == 0, c == 7, after=[t_wo[dc], bfree[b]])
                bfree[b] = tt(ht[dc][:, cols], PSB[2 + b][:], ht[dc][:, cols], ALU.add, after=[t, t_h[dc]])
            if tg < 3:
                steps[tg]()
        t_b3pe = Tok(P.esem["pe"], P.ecnt["pe"])
        for i_ in range(3):
            gfree[i_] = [t_b3pe]
        chk("B3")

        c2(0, Tok(P.esem["dve"], P.ecnt["dve"]), {})
        t_c2pe = Tok(P.esem["pe"], P.ecnt["pe"])
        steps1, st1 = make_c1(1, all_toks(), [t_c2pe])
        steps1[0]()
        c3(0, {0: steps1[1], 1: steps1[2]})
        steps4 = make_c4(0)
        steps4[0]()
        c2(1, st1["t_ft"], {0: steps4[1], 1: steps4[2]})
        t_pre = all_toks()
        fstats = dict(n=0, last=[None, None], pre=t_pre)
        P.wait("act", t_pre)
        c3(1, {}, stats=fstats)
        for tl_ in range(2):
            tg = 2 + tl_
            cols = slice(tg * 512, tg * 512 + 512)
            t_ln = act(rs2[tl_], PSB[tl_][:], AF.Ln, after=[fstats["last"][tl_], t_cst], scale=1.0 / D, bias=cstv[:, 96:97])
            t_ex = act(rs2[tl_], rs2[tl_], AF.Exp, after=[t_ln], scale=-0.5)
            for c in range(8):
                k16 = tl_ * 8 + c
                if k16 < 4:
                    slot, dsem_, pre_ = ost[k16], d_out[k16], [ostfree[k16]]
                else:
                    slot, dsem_, pre_ = f32(WGU + (k16 - 4) * 512, 512), d_out[k16 % 4], []
                t_o = stt(slot, ht[c][:, cols], cstv[:, 16 + c:17 + c], rs2[tl_], ALU.mult, ALU.mult,
                          after=[t_ex] + pre_)
                out_toks.append(P.dma("sp", oT[c * 128:(c + 1) * 128, cols], slot, dsem_, after=[t_o]))
        best = {}
        for t_ in out_toks:
            if id(t_.sem) not in best or best[id(t_.sem)].val < t_.val:
                best[id(t_.sem)] = t_
        P.wait("sp", list(best.values()))


_NC_CACHE = {}


def _host_consts(hf):
    f32 = np.float32
    inv_freq = 500000.0 ** (-(np.arange(0, 16, 2, dtype=np.float64)) / 16.0)
    pos = np.concatenate([np.arange(2048) + (hf - 1) * 2048, np.arange(2048) + hf * 2048]).astype(np.float64)
    ang = pos[None, :] * inv_freq[:, None]
    cs, sn = np.cos(ang).astype(f32), np.sin(ang).astype(f32)
    tab = np.zeros((128, 8192), f32)
    for p in range(128):
        dim = p % 64
        if dim < 8:
            tab[p, 0:4096] = cs[dim]
            tab[p, 4096:] = -sn[dim]
        elif dim < 16:
            tab[p, 0:4096] = cs[dim - 8]
            tab[p, 4096:] = sn[dim - 8]
        else:
            tab[p, 0:4096] = 1.0
    k = np.arange(128)[:, None]
    q = np.arange(128)[None, :]
    mprev = (q <= k).astype(f32)
    mdiag = (q >= k).astype(f32)
    msk = np.concatenate([mprev, mdiag, mprev, mdiag, mprev * hf, mdiag, mprev * hf, mdiag], axis=1).astype(f32)
    prm = np.zeros((128, 128), f32)
    for m in range(128):
        dim = m % 64
        if dim < 8:
            prm[m + 8, m] = 1.0
        elif dim < 16:
            prm[m - 8, m] = 1.0
    return tab, msk, prm


def make_in_maps(x, norm_mix, w_in, w_pool_group, pool_scale, w_att_branch, w_pool_branch,
                 w_out, norm_ffn, w_ffn_gate, w_ffn_up, w_ffn_down, norm_final):
    f32 = np.float32
    x = np.asarray(x, f32)
    w_in0 = np.asarray(w_in, f32)[0]
    cols = []
    for sp in range(4):
        for g in range(3):
            for typ in range(3):
                base = typ * 1536 + g * 512 + sp * 128
                cols.append(np.arange(base, base + 128))
    cols.append(np.arange(4608, 7168))
    cols = np.concatenate(cols)
    w_in_r = np.ascontiguousarray(w_in0[:, cols])
    wg = np.asarray(w_ffn_gate, f32)[0].reshape(D, NJ, 128)
    wu = np.asarray(w_ffn_up, f32)[0].reshape(D, NJ, 128)
    wgu = np.ascontiguousarray(np.stack([wg, wu], axis=2).reshape(D, 2 * DFF))
    wd = np.ascontiguousarray(np.asarray(w_ffn_down, f32)[0])
    wpg = np.ascontiguousarray(np.asarray(w_pool_group, f32)[0].transpose(1, 0, 2).reshape(128, 512))
    wab = np.ascontiguousarray(np.asarray(w_att_branch, f32)[0])
    wpb = np.ascontiguousarray(np.asarray(w_pool_branch, f32)[0])
    wout = np.ascontiguousarray(np.asarray(w_out, f32)[0])

    in_maps = []
    for c in range(8):
        b, hf = c // 2, c % 2
        xT = np.zeros((D, 4096), f32)
        xT[:, 2048:] = x[b, hf * 2048:(hf + 1) * 2048, :].T
        if hf == 1:
            xT[:, :2048] = x[b, 0:2048, :].T
        tab, msk, prm = _host_consts(hf)
        cst = np.zeros((128, 128), f32)
        cst[:, 0:8] = np.asarray(norm_mix, f32)[0].reshape(8, 128).T
        cst[:, 8:16] = np.asarray(norm_ffn, f32)[0].reshape(8, 128).T
        cst[:, 16:24] = np.asarray(norm_final, f32).reshape(8, 128).T
        cst[:, 24:28] = np.asarray(pool_scale, f32)[0].reshape(4, 128).T
        cst[:, 96] = EPS
        for gz in range(4):
            w = 2 ** (gz + 1)
            posj = hf * 2048 + np.arange(16)
            cnt = np.minimum(posj + 1, w).astype(f32)
            cst[:, 28 + gz * 16:28 + gz * 16 + 16] = (f32(1.0) / cnt)[None, :]
        in_maps.append({"xT": xT, "w_in": w_in_r, "tab": tab, "cst": cst, "msk": msk, "prm": prm,
                        "wpg": wpg, "wab": wab, "wpb": wpb, "wout": wout, "wgu": wgu, "wd": wd})
    return in_maps


def kernel(x, norm_mix, w_in, w_pool_group, pool_scale, w_att_branch, w_pool_branch,
           w_out, norm_ffn, w_ffn_gate, w_ffn_up, w_ffn_down, norm_final):
    f32 = np.float32
    in_maps = make_in_maps(x, norm_mix, w_in, w_pool_group, pool_scale, w_att_branch, w_pool_branch,
                           w_out, norm_ffn, w_ffn_gate, w_ffn_up, w_ffn_down, norm_final)
    if "nc" not in _NC_CACHE:
        _NC_CACHE["nc"] = build_program()
    res = run_bass_kernel_spmd(_NC_CACHE["nc"], in_maps, core_ids=list(range(8)))
    out = np.empty((4, S, D), f32)
    for c in range(8):
        b, hf = c // 2, c % 2
        out[b, hf * 2048:(hf + 1) * 2048, :] = res.results[c]["oT"].T
    return out
```

```python
import numpy as np
import concourse.bass as bass
import concourse.mybir as mybir
from concourse.bass_utils import run_bass_kernel_spmd
from contextlib import ExitStack

F32 = mybir.dt.float32
BF16 = mybir.dt.bfloat16
ALU = mybir.AluOpType
AF = mybir.ActivationFunctionType

D = 1024
S = 4096
NT = 2048
DFF = 2816
NJ = 22
EPS = 1e-6
GROUPS = ((128, 1), (512, 4), (2048, 16))
NBLK = (17, 20, 32)
KT_OFF = (0, 2176, 4736)
VS_OFF = (0, 17 * 192, 37 * 192)

NW = 52512
UT_H, UT_O, ATT = 0, 8192, 16384
CST, MSK, PRM, ONE = 20480, 20608, 21120, 21184
R = 21504
TAB, QT, KT, VS, ACC, WP, PT, QRAW, T12 = R, 25600, 28672, 33088, 39712, 43808, 48416, 49440, 49952
XS, SQ, RS = 25600, 33792, 35840
POOLT, MERGED, TMPB = 29696, 33792, 41984
ACTT, RS2 = 8192, 19456
FT, WGU, WD, SQ2, SIL, OST = 29696, 33792, 39936, 46080, 48128, 49152


class Tok:
    __slots__ = ("sem", "val")

    def __init__(self, sem, val):
        self.sem, self.val = sem, val


class DSem:
    def __init__(self, sem):
        self.sem, self.cnt = sem, 0


class Prog:
    ENGS = ("pe", "act", "dve", "pool", "sp")
    CE = ("pe", "act", "dve")

    def __init__(self, nc, es):
        self.nc, self.es = nc, es
        self.ops = {e: [] for e in self.ENGS}
        self.esem = {e: es.enter_context(nc.semaphore("es_" + e)) for e in self.CE}
        self.ecnt = {e: 0 for e in self.CE}
        self.seen = {e: {} for e in self.ENGS}
        self.nsem = 0
        self.dma_out = []

    def dsem(self):
        self.nsem += 1
        return DSem(self.es.enter_context(self.nc.semaphore(f"ds{self.nsem}")))

    def _waits(self, eng, after):
        waits = []
        for t in after:
            if t is None:
                continue
            k = id(t.sem)
            if self.seen[eng].get(k, 0) >= t.val:
                continue
            self.seen[eng][k] = t.val
            waits.append((t.sem, t.val))
        return waits

    def op(self, eng, fn, after=()):
        waits = self._waits(eng, after)
        self.ecnt[eng] += 1
        self.ops[eng].append((waits, fn, (self.esem[eng], 1)))
        return Tok(self.esem[eng], self.ecnt[eng])

    def dma(self, q, out, in_, ds, after=()):
        waits = self._waits(q, after)
        ds.cnt += 16
        self.ops[q].append((waits, lambda e: e.dma_start(out=out, in_=in_), (ds.sem, 16)))
        t = Tok(ds.sem, ds.cnt)
        self.dma_out.append(t)
        return t

    def wait(self, eng, after):
        waits = self._waits(eng, after)
        if waits:
            self.ops[eng].append((waits, None, None))

    def barrier(self):
        best = {}
        for t in self.dma_out:
            if id(t.sem) not in best or best[id(t.sem)].val < t.val:
                best[id(t.sem)] = t
        toks = [Tok(self.esem[e], self.ecnt[e]) for e in self.CE if self.ecnt[e] > 0] + list(best.values())
        self.dma_out = []
        for e in self.ENGS:
            self.wait(e, toks)

    def emit(self):
        with self.nc.Block() as block:
            for e, deco in (("pe", block.tensor), ("act", block.scalar), ("dve", block.vector),
                            ("pool", block.gpsimd), ("sp", block.sync)):
                ops = self.ops[e]

                def body(eng, ops=ops):
                    for waits, fn, inc in ops:
                        for s, v in waits:
                            eng.wait_ge(s, v)
                        if fn is not None:
                            ins = fn(eng)
                            if inc is not None:
                                ins.then_inc(inc[0], inc[1])

                deco(body)


def cm_view(ap, g, m):
    if g == 0:
        return ap[:, 512 * m:512 * m + 512]
    if g == 1:
        return ap[:, m:2048:4]
    return ap.rearrange("p (i r) -> p r i", r=16)[:, 4 * m:4 * m + 4, :]


def cont_view(ap, g):
    if g == 2:
        return ap.rearrange("p (r i) -> p r i", r=4)
    return ap


class _Stop(Exception):
    pass


DEBUG = {"stop": None, "dump": (0, 16)}


def build_program():
    nc = bass.Bass("TRN2", target_bir_lowering=False)
    dr = lambda n, s, k="ExternalInput": nc.dram_tensor(n, list(s), F32, kind=k).ap()
    xT = dr("xT", [D, 4096])
    w_in = dr("w_in", [D, 7168])
    tab = dr("tab", [128, 8192])
    cst = dr("cst", [128, 128])
    msk = dr("msk", [128, 1024])
    prm = dr("prm", [128, 128])
    wpg = dr("wpg", [128, 512])
    wab = dr("wab", [512, D])
    wpb = dr("wpb", [512, D])
    wout = dr("wout", [D, D])
    wgu = dr("wgu", [D, 2 * DFF])
    wd = dr("wd", [DFF, D])
    oT = dr("oT", [D, NT], "ExternalOutput")
    if DEBUG["stop"] is not None:
        dbg = dr("dbg", [128, DEBUG["dump"][1]], "ExternalOutput")
    else:
        dbg = None

    with ExitStack() as es:
        P = Prog(nc, es)
        try:
            _body(nc, es, P, locals())
        except _Stop:
            pass
        P.emit()
    return nc


def _body(nc, es, P, L):
    xT, w_in, tab, cst, msk, prm, wpg, wab, wpb, wout, wgu, wd, oT = [
        L[k] for k in ("xT", "w_in", "tab", "cst", "msk", "prm", "wpg", "wab", "wpb", "wout", "wgu", "wd", "oT")]
    dbg = L["dbg"]

    def chk(name):
        if DEBUG["stop"] == name:
            P.barrier()
            o, n = DEBUG["dump"]
            t = P.dma("sp", dbg, AR[:, o:o + n], P.dsem())
            P.wait("sp", [t])
            raise _Stop()
    if True:
        AR = es.enter_context(nc.sbuf_tensor("arena", [128, NW], F32))
        PS_ALL = es.enter_context(nc.psum_tensor("ps_all", [128, 4096], F32))
        PSB = [PS_ALL[:, i * 512:(i + 1) * 512] for i in range(8)]

        def f32(off, n):
            return AR[:, off:off + n]

        def bf(off, n):
            return AR[:, off:off + n // 2].bitcast(BF16)

        def mm(out, lhsT, rhs, start, stop, after=()):
            return P.op("pe", lambda e: e.matmul(out, lhsT=lhsT, rhs=rhs, start=start, stop=stop), after)

        def act(out, in_, func, after=(), **kw):
            return P.op("act", lambda e: e.activation(out=out, in_=in_, func=func, **kw), after)

        def tt(out, in0, in1, op, after=()):
            return P.op("dve", lambda e: e.tensor_tensor(out=out, in0=in0, in1=in1, op=op), after)

        def stt(out, in0, scalar, in1, op0, op1, after=()):
            return P.op("dve", lambda e: e.scalar_tensor_tensor(out=out, in0=in0, scalar=scalar, in1=in1,
                                                                op0=op0, op1=op1), after)

        uth = bf(UT_H, 8 * 2048).rearrange("p (c t) -> p c t", c=8)
        uto = bf(UT_O, 8 * 2048).rearrange("p (c t) -> p c t", c=8)
        att = bf(ATT, 4 * 2048).rearrange("p (c t) -> p c t", c=4)
        cstv = f32(CST, 128)
        mskv = bf(MSK, 1024)
        prmv = bf(PRM, 128)
        onev = bf(ONE, 128)
        tabv = bf(TAB, 8192)
        tcos_h, tcos_o = tabv[:, 0:2048], tabv[:, 2048:4096]
        tsin_h, tsin_o = tabv[:, 4096:6144], tabv[:, 6144:8192]

        t_cst = P.dma("sp", cstv, cst, P.dsem())
        t_msk = P.dma("pool", mskv, msk, P.dsem())
        t_prm = P.dma("pool", prmv, prm, P.dsem())
        t_ms1 = P.op("dve", lambda e: e.memset(tabv[:, 0:4096], 1.0))
        t_ms2 = P.op("dve", lambda e: e.memset(tabv[:, 4096:8192], 0.0))
        d_tab = P.dsem()
        for p0 in (0, 64):
            t_tab = P.dma("pool", tabv[p0:p0 + 16, :].rearrange("p (a b) -> p a b", b=1024),
                          tab[p0:p0 + 16, :].rearrange("p (a b) -> p a b", b=1024), d_tab, after=[t_ms1, t_ms2])
        t_one = P.op("dve", lambda e: e.memset(onev, 1.0))
        t_const = [t_cst, t_msk, t_prm, t_tab, t_one]

        wpv = bf(WP, 8 * 1152).rearrange("p (c n) -> p c n", c=8)
        w_inv = w_in.rearrange("(c p) n -> p c n", p=128)
        d_wp = P.dsem()
        t_wp0 = P.dma("pool", wpv, w_inv[:, :, 0:1152], d_wp)

        def rms_stats(src_fn, sqv, rsv, psb, after, rs_after=()):
            ts = []
            after = list(after) + [Tok(P.esem["pe"], P.ecnt["pe"])]
            for c in range(8):
                ts.append(act(sqv[:, c, :], src_fn(c), AF.Square, after=after))
            t = None
            for c in range(8):
                t = mm(psb[:], onev, sqv[:, c, :], c == 0, c == 7, after=[ts[c], t_one] + list(after))
            t = act(rsv, psb[:], AF.Ln, after=[t, t_cst] + list(rs_after), scale=1.0 / D, bias=cstv[:, 96:97])
            t = act(rsv, rsv, AF.Exp, after=[t], scale=-0.5)
            return t

        xTv = xT.rearrange("(c p) t -> p c t", p=128)
        xs = [f32(XS + i * 4096, 4096).rearrange("p (c t) -> p c t", c=8) for i in range(3)]
        sqv2 = [bf(37888 + j * 2048, 4096).rearrange("p (c t) -> p c t", c=8) for j in range(2)]
        rsv2 = [f32(41984, 512), f32(42496, 512)]
        d_x = [P.dsem(), P.dsem(), P.dsem()]
        xs_free = [[], [], []]
        rs_free = [[], []]
        for tg in range(8):
            i = tg % 3
            j = tg % 2
            t_ld = P.dma("sp", xs[i], xTv[:, :, tg * 512:(tg + 1) * 512], d_x[i], after=xs_free[i])
            t_rs = rms_stats(lambda c: xs[i][:, c, :], sqv2[j], rsv2[j], PSB[tg % 2], [t_ld], rs_after=rs_free[j])
            dst = uth if tg < 4 else uto
            col = (tg % 4) * 512
            tl = []
            for c in range(8):
                tl.append(stt(dst[:, c, col:col + 512], xs[i][:, c, :], cstv[:, c:c + 1], rsv2[j],
                              ALU.mult, ALU.mult, after=[t_rs, t_cst]))
            xs_free[i] = [tl[-1]]
            rs_free[j] = [tl[-1]]
        P.barrier()
        chk("P0")

        qtv = [bf(QT + g * 1024, 2048) for g in range(3)]
        ktv = [bf(KT, 8832)[:, KT_OFF[g]:KT_OFF[g] + NBLK[g] * 128] for g in range(3)]
        vsv = [bf(VS, 69 * 192)[:, VS_OFF[g]:VS_OFF[g] + NBLK[g] * 192].rearrange("p (b k) -> p b k", k=192)
               for g in range(3)]
        accv = [f32(ACC + e * 2048, 2048) for e in range(2)]
        ptv = [bf(PT + i * 256, 512) for i in range(4)] + [bf(52000 + i * 256, 512) for i in range(2)]
        qrawv = [bf(QRAW + i * 256, 512) for i in range(2)]
        t1v = [f32(T12 + i * 512, 512) for i in range(2)]
        t2v = [f32(T12 + 1024 + i * 512, 512) for i in range(2)]
        ntmp = f32(T12, 2048)

        vs_all = bf(VS, 69 * 192).rearrange("p (b k) -> p b k", k=192)
        P.op("dve", lambda e: e.memset(vs_all[:, :, 64:128], 1.0))
        t_wp = t_wp0

        XS_ = dict(psq=[None, None], pss=[None, None], v=[None] * 4, t1=[None, None], tpe=None)
        for sp in range(4):
            units = []

            def cmdst(region, d, m):
                if d == 1:
                    return region[:, 512 * m:512 * m + 512]
                w = 512 // d
                return region.rearrange("p (r i) -> p r i", r=d)[:, :, w * m:w * m + w]

            for g in range(3):
                d = GROUPS[g][1]
                for typ in (0, 1):
                    wcol = (g * 3 + typ) * 128
                    dstb = qtv[g] if typ == 0 else ktv[g][:, 0:2048]
                    for m in range(4):
                        units.append((cmdst(dstb, d, m), wcol,
                                      (lambda c, m=m: uto[:, c, 512 * m:512 * m + 512]),
                                      tcos_o[:, 512 * m:512 * m + 512], tsin_o[:, 512 * m:512 * m + 512], 512, d))
                wcol = (g * 3 + 1) * 128
                if g == 2:
                    for m in range(4):
                        units.append((cmdst(ktv[g][:, 2048:4096], 16, m), wcol,
                                      (lambda c, m=m: uth[:, c, 512 * m:512 * m + 512]),
                                      tcos_h[:, 512 * m:512 * m + 512], tsin_h[:, 512 * m:512 * m + 512], 512, 16))
                elif g == 1:
                    units.append((ktv[g][:, 2048:2560].rearrange("p (r i) -> p r i", r=4), wcol,
                                  (lambda c: uth[:, c, 1536:2048]), tcos_h[:, 1536:2048], tsin_h[:, 1536:2048], 512, 4))
                else:
                    units.append((ktv[g][:, 2048:2176], wcol, (lambda c: uth[:, c, 1920:2048]),
                                  tcos_h[:, 1920:2048], tsin_h[:, 1920:2048], 128, 1))

            P.wait("pe", [t_wp])
            chk(f"A{sp}Pw")
            if DEBUG.get("units") is not None:
                units = [units[k] for k in DEBUG["units"]]
            v_batches = []
            for g in range(3):
                d = GROUPS[g][1]
                blocks = []
                for b in range(16):
                    r, qb = divmod(b, 16 // d)
                    blocks.append((b, False, r + d * 128 * qb))
                for r in range(d):
                    blocks.append((16 + r, True, 2048 - 128 * d + r))
                for b0 in range(0, len(blocks), 4):
                    v_batches.append((g, blocks[b0:b0 + 4]))
            vb_i = [0]
            vfree = list(XS_["v"])

            def emit_vbatch():
                vbank = vb_i[0]
                g, batch = v_batches[vbank]
                vb_i[0] += 1
                d = GROUPS[g][1]
                wcol = (g * 3 + 2) * 128
                nb = len(batch)
                psv = PSB[4 + vbank % 4]
                t = None
                for k, (b, hist, s0) in enumerate(batch):
                    src = uth if hist else uto
                    for c in range(8):
                        t = mm(psv[:, k * 128:(k + 1) * 128], src[:, c, s0:s0 + 127 * d + 1:d],
                               wpv[:, c, wcol:wcol + 128], c == 0, c == 7, after=[t_wp, vfree[vbank % 4]])
                bfirst = batch[0][0]
                outv = vsv[g][:, bfirst:bfirst + nb, :].rearrange("p b (h k) -> p b h k", k=64)[:, :, 0:3:2, :]
                inv = psv[:, 0:nb * 128].rearrange("p (b h k) -> p b h k", b=nb, h=2)
                vfree[vbank % 4] = act(outv, inv, AF.Copy, after=[t, XS_["tpe"]])

            psq_free = list(XS_["psq"])
            pss_free = list(XS_["pss"])
            qraw_free = [None, None]
            t1_free = list(XS_["t1"])
            pend = None

            def do_perm(pd):
                (ui, t_raw, i, N, shp, dst, cosv, sinv, t_t1) = pd
                pss = PSB[2 + i]
                LV = DEBUG.get("lv", 9)
                if LV < 3:
                    return
                t_p = mm(pss[:, 0:N], prmv, qrawv[i][:, 0:N], True, True, after=[t_raw, pss_free[i], t_prm])
                if LV < 4:
                    return
                t_t2 = tt(t2v[i][:, 0:N], pss[:, 0:N], sinv, ALU.mult, after=[t_p, t_tab, t1_free[i]])
                if LV < 5:
                    return
                t_o = tt(dst, shp(t1v[i][:, 0:N]), shp(t2v[i][:, 0:N]), ALU.add, after=[t_t1, t_t2, XS_["tpe"]])
                pss_free[i] = t_t2
                qraw_free[i] = t_p
                t1_free[i] = t_o

            for ui, (dst, wcol, rhs_fn, cosv, sinv, N, dd) in enumerate(units):
                i = ui % 2
                psq = PSB[i]
                shp = (lambda a, dd=dd: a.rearrange("p (i r) -> p r i", r=dd)) if dd > 1 else (lambda a: a)
                t = None
                for c in range(8):
                    t = mm(psq[:, 0:N], wpv[:, c, wcol:wcol + 128], rhs_fn(c), c == 0, c == 7,
                           after=[t_wp, psq_free[i]])
                t_raw = act(qrawv[i][:, 0:N], psq[:, 0:N], AF.Copy, after=[t, qraw_free[i]])
                t_t1 = tt(t1v[i][:, 0:N], psq[:, 0:N], cosv, ALU.mult, after=[t, t_raw, t1_free[i], t_tab])
                psq_free[i] = t_t1
                if pend is not None:
                    do_perm(pend)
                pend = (ui, t_raw, i, N, shp, dst, cosv, sinv, t_t1)
                while vb_i[0] < len(v_batches) and vb_i[0] * len(units) < (ui + 1) * len(v_batches):
                    emit_vbatch()
            do_perm(pend)
            chk(f"A{sp}Pqk")

            while vb_i[0] < len(v_batches):
                emit_vbatch()
            t_Pdone = [Tok(P.esem["dve"], P.ecnt["dve"]), Tok(P.esem["act"], P.ecnt["act"])]
            t_Ppe = Tok(P.esem["pe"], P.ecnt["pe"])
            chk(f"A{sp}P")
            if sp < 3:
                t_wp = P.dma("pool", wpv, w_inv[:, :, (sp + 1) * 1152:(sp + 2) * 1152], d_wp, after=[t_Ppe])
            else:
                zw = bf(WP, 4096).rearrange("p (c n) -> p c n", c=8)
                wpgv = bf(WP + 2048, 512)
                d_b = P.dsem()
                P.dma("pool", zw, w_inv[:, :, 4608:5120], d_b, after=[t_Ppe])
                t_wpg = P.dma("pool", wpgv, wpg, d_b)
                t_zw = t_wpg

            tiles = []
            nbatch = 0
            for m in range(4):
                lst = []
                for k in range(4):
                    b = 4 * m + k
                    lst.append((0, b, (b - 1) if b >= 1 else 16, b >= 1, slice(k * 128, k * 128 + 128)))
                for r4 in range(4):
                    b = r4 * 4 + m
                    lst.append((1, b, (b - 1) if m >= 1 else 16 + r4, m >= 1, slice(r4, 512, 4)))
                for j, (g_, b_, pb_, nm_, oc_) in enumerate(lst):
                    tiles.append((g_, b_, pb_, nm_, nbatch % 2, oc_, j == 0, j == len(lst) - 1, ("copy", m)))
                nbatch += 1
            for m in range(4):
                for k in range(4):
                    b = 4 * m + k
                    tiles.append((2, b, 16 + b, False, nbatch % 2, slice(k * 128, k * 128 + 128), k == 0, k == 3,
                                  ("add2", m)))
                nbatch += 1
            sfree = [[psq_free[0], psq_free[1]], [pss_free[0], pss_free[1]]]
            ptfree = [None] * 6
            ofree = [[vfree[0], vfree[1]], [vfree[2], vfree[3]]]
            pend_pv = []
            acc_last = [None, None]

            def do_pv(item):
                (ti, tile, t_m) = item
                (g, b, prevb, normal, par, oc, first, last, evac) = tile
                i4 = ti % 6
                tl = None
                for e in range(2):
                    pso = PSB[4 + 2 * e + par]
                    o = pso[:, oc]
                    P.op("pe", lambda en, o=o, l=vsv[g][:, prevb, e * 64:e * 64 + 128], r=ptv[i4][:, e * 256:e * 256 + 128],
                         st=first: en.matmul(o, lhsT=l, rhs=r, start=st, stop=False, skip_group_check=True),
                         [t_m, ofree[e][par]])
                    tl = P.op("pe", lambda en, o=o, l=vsv[g][:, b, e * 64:e * 64 + 128],
                              r=ptv[i4][:, e * 256 + 128:e * 256 + 256], sp_=last:
                              en.matmul(o, lhsT=l, rhs=r, start=False, stop=sp_, skip_group_check=True))
                ptfree[i4] = tl
                if last:
                    kind, m = evac
                    for e in range(2):
                        pso = PSB[4 + 2 * e + par]
                        if kind == "copy":
                            ofree[e][par] = act(accv[e][:, 512 * m:512 * m + 512], pso[:], AF.Copy,
                                                after=[tl, XS_["t1"][0]])
                        else:
                            av = cm_view(accv[e], 2, m)
                            ofree[e][par] = tt(av, cont_view(pso[:], 2), av, ALU.add, after=[tl, acc_last[e]])
                        acc_last[e] = ofree[e][par]

            for ti, tile in enumerate(tiles):
                (g, b, prevb, normal) = tile[0:4]
                i4 = ti % 6
                i2 = ti % 2
                t = None
                for kb_, c0_ in ((prevb, 0), (b, 128)):
                    for e in range(2):
                        ps = PSB[2 * i2 + e]
                        pr = slice(64 * e, 64 * e + 64)
                        q = qtv[g][pr, b * 128:(b + 1) * 128]
                        t = mm(ps[:, c0_:c0_ + 128], ktv[g][pr, kb_ * 128:(kb_ + 1) * 128], q, True, True,
                               after=list(sfree[i2]) + t_Pdone)
                ps2 = PS_ALL[:, 2 * i2 * 512:(2 * i2 + 2) * 512].rearrange("p (e c) -> p e c", e=2)[:, :, 0:256]
                t_e = act(ptv[i4].rearrange("p (e c) -> p e c", e=2), ps2, AF.Exp, after=[t, ptfree[i4]], scale=0.125)
                sfree[i2] = [t_e]
                mv = mskv[:, 0:512] if normal else mskv[:, 512:1024]
                t_m = tt(ptv[i4], ptv[i4], mv, ALU.mult, after=[t_e, t_msk])
                pend_pv.append((ti, tile, t_m))
                if len(pend_pv) > 4:
                    do_pv(pend_pv.pop(0))
            while pend_pv:
                do_pv(pend_pv.pop(0))
            tU = Tok(P.esem["dve"], P.ecnt["dve"])
            tA = Tok(P.esem["act"], P.ecnt["act"])
            t_c0 = act(ntmp[0:64, :], accv[0][64:128, :], AF.Ln, after=[tU, tA])
            t_c1 = act(ntmp[64:128, :], accv[1][0:64, :], AF.Ln, after=[tU, tA])
            t_l = act(ntmp, ntmp, AF.Exp, after=[t_c0, t_c1], scale=-1.0)
            tt(att[0:64, sp, :], accv[0][0:64, :], ntmp[0:64, :], ALU.mult, after=[t_l])
            t_nrm = tt(att[64:128, sp, :], accv[1][64:128, :], ntmp[64:128, :], ALU.mult, after=[t_l])
            if sp == 3:
                P.barrier()
            else:
                XS_["psq"] = [sfree[0][0], sfree[0][0]]
                XS_["pss"] = [sfree[1][0], sfree[1][0]]
                XS_["v"] = [ofree[0][0], ofree[0][1], ofree[1][0], ofree[1][1]]
                XS_["t1"] = [t_nrm, t_nrm]
                XS_["tpe"] = Tok(P.esem["pe"], P.ecnt["pe"])
            chk(f"A{sp}")

        ht = [f32(UT_H + c * 2048, 2048) if c < 4 else f32(R + (c - 4) * 2048, 2048) for c in range(8)]
        d_h = P.dsem()
        t_h = [None] * 8
        for c in range(4, 8):
            t_h[c] = P.dma("sp", ht[c], xT[c * 128:(c + 1) * 128, 2048:4096], d_h)
        poolt = bf(POOLT, 4 * 2048).rearrange("p (c t) -> p c t", c=4)
        merged = bf(MERGED, 8 * 2048).rearrange("p (c t) -> p c t", c=8)
        pre = [bf(TMPB + i * 256, 512) for i in range(4)]
        o = WP + 2048 + 256
        zbuf = [[f32(o + (2 * i + k) * 528, 528) for k in range(2)] for i in range(4)]; o += 4224
        ta, tb = f32(o, 528), f32(o + 528, 528); o += 1056
        t16 = f32(o, 16); o += 16
        assert o <= NW
        halo_tok = [None] * 4
        psz_free = [None] * 4
        for gz in range(4):
            t = None
            for c in range(8):
                t = mm(PSB[gz][:, 0:16], zw[:, c, gz * 128:(gz + 1) * 128], uth[:, c, 2032:2048], c == 0, c == 7,
                       after=[t_zw])
            halo_tok[gz] = act(zbuf[gz][0][:, 0:16], PSB[gz][:, 0:16], AF.Copy, after=[t])
            psz_free[gz] = halo_tok[gz]
        t_halo = Tok(P.esem["pe"], P.ecnt["pe"])
        for c in range(4):
            t_h[c] = P.dma("sp", ht[c], xT[c * 128:(c + 1) * 128, 2048:4096], d_h, after=[t_halo])
        t_h = [t_h[3]] * 8
        pre_free = [None] * 4
        ppfree = [None] * 4
        zb_free = [[None, None] for _ in range(4)]
        b1_pend = []
        tab_last = [None]

        def b1_pool(item):
            gz, cols, t_p = item
            psp = PSB[4 + gz]
            t_m = mm(psp[:], wpgv[:, gz * 128:(gz + 1) * 128], pre[gz], True, True,
                     after=[t_p, t_wpg, ppfree[gz]])
            pre_free[gz] = t_m
            ppfree[gz] = act(poolt[:, gz, cols], psp[:], AF.Copy, after=[t_m], scale=cstv[:, 24 + gz:25 + gz])
        for tg in range(4):
            cols = slice(tg * 512, tg * 512 + 512)
            kb = tg % 2
            for gz in range(4):
                psz = PSB[gz]
                z = zbuf[gz][kb]
                zn = zbuf[gz][1 - kb]
                t = None
                for c in range(8):
                    t = mm(psz[:], zw[:, c, gz * 128:(gz + 1) * 128], uto[:, c, cols], c == 0, c == 7,
                           after=[t_zw, psz_free[gz]])
                t_z = act(z[:, 16:528], psz[:], AF.Copy, after=[t, zb_free[gz][kb]])
                psz_free[gz] = t_z
                t_hn = act(zn[:, 0:16], z[:, 512:528], AF.Copy, after=[t_z, zb_free[gz][1 - kb]])
                w = 2 ** (gz + 1)
                cur, lo, t_c = z, 0, t_z
                bufs = [ta, tb]
                sh = 1
                for lvl in range(gz + 1):
                    nxt = bufs[lvl % 2]
                    nlo = lo + sh
                    t_c = tt(nxt[:, nlo:528], cur[:, nlo:528], cur[:, nlo - sh:528 - sh], ALU.add,
                             after=[t_c, halo_tok[gz], pre_free[gz], tab_last[0]] if lvl == 0 else [t_c])
                    cur, lo = nxt, nlo
                    sh *= 2
                t_p = stt(pre[gz], cur[:, 16:528], 1.0 / w, z[:, 16:528], ALU.mult, ALU.subtract,
                          after=[t_c, pre_free[gz]])
                if tg == 0:
                    t_f = tt(t16, cur[:, 16:32], cstv[:, 28 + gz * 16:28 + gz * 16 + 16], ALU.mult, after=[t_p])
                    t_p = tt(pre[gz][:, 0:16], t16, z[:, 16:32], ALU.subtract, after=[t_f])
                zb_free[gz][kb] = t_p
                tab_last[0] = t_p
                halo_tok[gz] = t_hn
                b1_pend.append((gz, cols, t_p))
                if len(b1_pend) > 2:
                    b1_pool(b1_pend.pop(0))
        while b1_pend:
            b1_pool(b1_pend.pop(0))
        P.barrier()
        chk("B1")
        o = TMPB
        ring = []
        for i in range(2):
            ga = bf(o, 1024).rearrange("p (c n) -> p c n", c=8); o += 512
            gp = bf(o, 1024).rearrange("p (c n) -> p c n", c=8); o += 512
            ab = bf(o, 512).rearrange("p (c n) -> p c n", c=4); o += 256
            pb = bf(o, 512).rearrange("p (c n) -> p c n", c=4); o += 256
            ring.append((ga, gp, ab, pb))
        sig = [[f32(o + (2 * i + k) * 512, 512) for k in range(2)] for i in range(2)]; o += 2048
        m12 = [[f32(o + (2 * i + k) * 512, 512) for k in range(2)] for i in range(2)]; o += 2048
        wabv = wab.rearrange("(c p) n -> p c n", p=128)
        wpbv = wpb.rearrange("(c p) n -> p c n", p=128)
        d_r = [P.dsem(), P.dsem()]
        rfree = [[], []]
        sgfree = [None, None]
        mfree = [None, None]
        pfree = [[None] * 4, [None] * 4]
        it = 0
        for dc in range(8):
            i = dc % 2
            ga, gp, ab, pb = ring[i]
            cs = slice(dc * 128, dc * 128 + 128)
            P.dma("pool", ga, w_inv[:, :, 5120 + dc * 128:5120 + dc * 128 + 128], d_r[i], after=rfree[i])
            P.dma("pool", gp, w_inv[:, :, 6144 + dc * 128:6144 + dc * 128 + 128], d_r[i])
            P.dma("pool", ab, wabv[:, :, cs], d_r[i])
            t_w = P.dma("pool", pb, wpbv[:, :, cs], d_r[i])
            for tg in range(4):
                cols = slice(tg * 512, tg * 512 + 512)
                s = it % 2
                it += 1
                bA, bP, bYA, bYP = [PSB[4 * s + k] for k in range(4)]
                tA = tP = tYA = tYP = None
                for c in range(8):
                    tA = mm(bA[:], ga[:, c, :], uto[:, c, cols], c == 0, c == 7, after=[t_w, pfree[s][0]])
                for c in range(8):
                    tP = mm(bP[:], gp[:, c, :], uto[:, c, cols], c == 0, c == 7, after=[pfree[s][1]])
                for c in range(4):
                    tYA = mm(bYA[:], ab[:, c, :], att[:, c, cols], c == 0, c == 3, after=[pfree[s][2]])
                for c in range(4):
                    tYP = mm(bYP[:], pb[:, c, :], poolt[:, c, cols], c == 0, c == 3, after=[pfree[s][3]])
                t_sa = act(sig[s][0], bA[:], AF.Sigmoid, after=[tA, sgfree[s]])
                t_sp = act(sig[s][1], bP[:], AF.Sigmoid, after=[tP, sgfree[s]])
                pfree[s][0], pfree[s][1] = t_sa, t_sp
                t_m1 = tt(m12[s][0], bYA[:], sig[s][0], ALU.mult, after=[tYA, t_sa, mfree[s]])
                t_m2 = tt(m12[s][1], bYP[:], sig[s][1], ALU.mult, after=[tYP, t_sp, mfree[s]])
                pfree[s][2], pfree[s][3] = t_m1, t_m2
                sgfree[s] = t_m2
                t_mg = tt(merged[:, dc, cols], m12[s][0], m12[s][1], ALU.add, after=[t_m1, t_m2])
                mfree[s] = t_mg
            rfree[i] = [Tok(P.esem["pe"], P.ecnt["pe"])]
        P.barrier()
        chk("B2")
        actt = bf(ACTT, NJ * 1024).rearrange("p (j t) -> p j t", j=NJ)
        rs2 = [f32(RS2, 512), f32(RS2 + 512, 512)]
        ftv = bf(FT, 8 * 1024).rearrange("p (c t) -> p c t", c=8)
        wguv = [bf(WGU + i * 2048, 4096).rearrange("p (c n) -> p c n", c=8) for i in range(3)]
        wdv = [bf(WD + i * 2816, 5632).rearrange("p (j n) -> p j n", j=NJ) for i in range(2)]
        sq2 = bf(SQ2, 4096).rearrange("p (c t) -> p c t", c=8)
        sil = [f32(SIL + i * 512, 512) for i in range(2)]
        ost = [f32(OST + i * 512, 512) for i in range(4)]
        wgu_d = wgu.rearrange("(c p) n -> p c n", p=128)
        wd_d = wd.rearrange("(j p) n -> p j n", p=128)
        d_g = [P.dsem() for _ in range(3)]
        d_d = [P.dsem() for _ in range(2)]
        d_out = [P.dsem() for _ in range(4)]
        gfree = [[], [], []]
        dfree = [[], []]
        ostfree = [None] * 4
        out_toks = []
        C = dict(gi=0, di=0, oi=0, ofree2=[None] * 4)

        def norm_a(src_fn, after):
            return [act(sq2[:, c, :], src_fn(c), AF.Square, after=after) for c in range(8)]

        def norm_b(ts, psb, rsv):
            t = None
            for c in range(8):
                t = mm(psb[:], onev, sq2[:, c, :], c == 0, c == 7, after=[ts[c], t_one])
            t_ln = act(rsv, psb[:], AF.Ln, after=[t, t_cst], scale=1.0 / D, bias=cstv[:, 96:97])
            t_ex = act(rsv, rsv, AF.Exp, after=[t_ln], scale=-0.5)
            return t, t_ln, t_ex

        def all_toks():
            return [Tok(P.esem[e], P.ecnt[e]) for e in ("pe", "act", "dve")]

        def make_c1(pp, first_after, ft_after):
            st = {}

            def src(tl_):
                tg = 2 * pp + tl_
                return lambda c: ht[c][:, tg * 512:tg * 512 + 512]

            def fin(tl_, t_ex):
                tg = 2 * pp + tl_
                for c in range(8):
                    st["t_ft"] = stt(ftv[:, c, tl_ * 512:tl_ * 512 + 512], ht[c][:, tg * 512:tg * 512 + 512],
                                     cstv[:, 8 + c:9 + c], rs2[tl_], ALU.mult, ALU.mult, after=[t_ex] + list(ft_after))

            def s0():
                st["ts"] = norm_a(src(0), first_after() if callable(first_after) else first_after)

            def s1():
                t_mm, t_ln, t_ex = norm_b(st["ts"], PSB[0], rs2[0])
                fin(0, t_ex)
                st["ts"] = norm_a(src(1), [t_mm, Tok(P.esem["dve"], P.ecnt["dve"])])

            def s2():
                t_mm, t_ln, t_ex = norm_b(st["ts"], PSB[1], rs2[1])
                fin(1, t_ex)
            return [s0, s1, s2], st

        def make_c4(pp):
            st = {}

            def src(tl_):
                tg = 2 * pp + tl_
                return lambda c: ht[c][:, tg * 512:tg * 512 + 512]

            def fin(tl_, t_ex):
                tg = 2 * pp + tl_
                cols = slice(tg * 512, tg * 512 + 512)
                for c in range(8):
                    k = C["oi"] % 4
                    C["oi"] += 1
                    t_o = stt(ost[k], ht[c][:, cols], cstv[:, 16 + c:17 + c], rs2[tl_], ALU.mult, ALU.mult,
                              after=[t_ex, ostfree[k]])
                    ostfree[k] = P.dma("sp", oT[c * 128:(c + 1) * 128, cols], ost[k], d_out[k], after=[t_o])
                    out_toks.append(ostfree[k])

            def s0():
                st["ts"] = norm_a(src(0), all_toks())

            def s1():
                t_mm, t_ln, t_ex = norm_b(st["ts"], PSB[4], rs2[0])
                C["ofree2"][0] = t_ln
                fin(0, t_ex)
                st["ts"] = norm_a(src(1), [t_mm])

            def s2():
                t_mm, t_ln, t_ex = norm_b(st["ts"], PSB[5], rs2[1])
                C["ofree2"][1] = t_ln
                fin(1, t_ex)
            return [s0, s1, s2]

        def c2(pp, t_ft, hooks):
            gufree = [[None, None], [None, None]]
            silfree = [None, None]
            it = 0
            for jb in range(11):
                if pp == 0 and jb == 0 and C.get("alt") is not None:
                    wblk, t_w, i = C["alt"][0], C["alt"][1], None
                else:
                    i = C["gi"] % 3
                    C["gi"] += 1
                    wblk = wguv[i]
                    t_w = P.dma("pool", wblk, wgu_d[:, :, jb * 512:(jb + 1) * 512], d_g[i], after=gfree[i])
                for jj in range(2):
                    j = 2 * jb + jj
                    for tl_ in range(2):
                        lc = slice(tl_ * 512, tl_ * 512 + 512)
                        s = it % 2
                        it += 1
                        bG, bU = PSB[2 * s], PSB[2 * s + 1]
                        tG = tU_ = None
                        for c in range(8):
                            tG = mm(bG[:], wblk[:, c, jj * 256:jj * 256 + 128], ftv[:, c, lc], c == 0, c == 7,
                                    after=[t_w, t_ft, gufree[s][0]])
                        for c in range(8):
                            tU_ = mm(bU[:], wblk[:, c, jj * 256 + 128:jj * 256 + 256], ftv[:, c, lc], c == 0, c == 7,
                                     after=[gufree[s][1]])
                        t_s = act(sil[s], bG[:], AF.Silu, after=[tG, silfree[s]])
                        gufree[s][0] = t_s
                        t_a = tt(actt[:, j, lc], bU[:], sil[s], ALU.mult, after=[tU_, t_s])
                        gufree[s][1] = t_a
                        silfree[s] = t_a
                if i is not None:
                    gfree[i] = [Tok(P.esem["pe"], P.ecnt["pe"])]
                if jb in hooks:
                    hooks[jb]()

        def c3(pp, hooks, stats=None):
            t_act = Tok(P.esem["dve"], P.ecnt["dve"])
            spend = []
            sq_free = [None] * 8

            def stat_mm():
                t_sq, k, tl_s, dc_s = spend.pop(0)
                sq_free[k] = mm(PSB[tl_s][:], onev, sq2[:, k, :], dc_s == 0, dc_s == 7,
                                after=[t_sq, t_one] + list(stats.get("pre", [])))
                stats["last"][tl_s] = sq_free[k]
            it = 0
            for db in range(4):
                i = C["di"] % 2
                C["di"] += 1
                t_w = P.dma("pool", wdv[i], wd_d[:, :, db * 256:(db + 1) * 256], d_d[i], after=dfree[i])
                for dd in range(2):
                    dc = 2 * db + dd
                    for tl_ in range(2):
                        tg = 2 * pp + tl_
                        cols = slice(tg * 512, tg * 512 + 512)
                        lc = slice(tl_ * 512, tl_ * 512 + 512)
                        b = it % 4
                        it += 1
                        t = None
                        for j in range(NJ):
                            t = mm(PSB[4 + b][:], wdv[i][:, j, dd * 128:dd * 128 + 128], actt[:, j, lc], j == 0, j == NJ - 1,
                                   after=[t_w, t_act, C["ofree2"][b]])
                        C["ofree2"][b] = tt(ht[dc][:, cols], PSB[4 + b][:], ht[dc][:, cols], ALU.add, after=[t])
                        if stats is not None:
                            k = stats["n"] % 8
                            stats["n"] += 1
                            t_sq = act(sq2[:, k, :], ht[dc][:, cols], AF.Square, after=[C["ofree2"][b], sq_free[k]])
                            spend.append((t_sq, k, tl_, dc))
                            if len(spend) > 2:
                                stat_mm()
                dfree[i] = [Tok(P.esem["pe"], P.ecnt["pe"])]
                if db in hooks:
                    hooks[db]()
            while spend:
                stat_mm()

        wo = [bf(TMPB + i * 512, 1024).rearrange("p (c n) -> p c n", c=8) for i in range(8)]
        woutv = wout.rearrange("(c p) n -> p c n", p=128)
        t_wo = [P.dma("pool", wo[dc], woutv[:, :, dc * 128:dc * 128 + 128], P.dsem()) for dc in range(8)]
        wgu_alt = bf(ACTT + 18 * 512, 4096).rearrange("p (c n) -> p c n", c=8)
        C["alt"] = (wgu_alt, P.dma("pool", wgu_alt, wgu_d[:, :, 0:512], P.dsem()))
        steps, st0 = make_c1(0, all_toks, [])
        bfree = [None] * 6
        it = 0
        for tg in range(4):
            cols = slice(tg * 512, tg * 512 + 512)
            for dc in range(8):
                b = it % 6
                it += 1
                t = None
                for c in range(8):
                    t = mm(PSB[2 + b][:], wo[dc][:, c, :], merged[:, c, cols], c == 0, c == 7, after=[t_wo[dc], bfree[b]])
                bfree[b] = tt(ht[dc][:, cols], PSB[2 + b][:], ht[dc][:, cols], ALU.add, after=[t, t_h[dc]])
            if tg < 3:
                steps[tg]()
        t_b3pe = Tok(P.esem["pe"], P.ecnt["pe"])
        for i_ in range(3):
            gfree[i_] = [t_b3pe]
        chk("B3")

        c2(0, Tok(P.esem["dve"], P.ecnt["dve"]), {})
        t_c2pe = Tok(P.esem["pe"], P.ecnt["pe"])
        steps1, st1 = make_c1(1, all_toks(), [t_c2pe])
        steps1[0]()
        c3(0, {0: steps1[1], 1: steps1[2]})
        steps4 = make_c4(0)
        steps4[0]()
        c2(1, st1["t_ft"], {0: steps4[1], 1: steps4[2]})
        t_pre = all_toks()
        fstats = dict(n=0, last=[None, None], pre=t_pre)
        P.wait("act", t_pre)
        c3(1, {}, stats=fstats)
        for tl_ in range(2):
            tg = 2 + tl_
            cols = slice(tg * 512, tg * 512 + 512)
            t_ln = act(rs2[tl_], PSB[tl_][:], AF.Ln, after=[fstats["last"][tl_], t_cst], scale=1.0 / D, bias=cstv[:, 96:97])
            t_ex = act(rs2[tl_], rs2[tl_], AF.Exp, after=[t_ln], scale=-0.5)
            for c in range(8):
                k16 = tl_ * 8 + c
                if k16 < 4:
                    slot, dsem_, pre_ = ost[k16], d_out[k16], [ostfree[k16]]
                else:
                    slot, dsem_, pre_ = f32(WGU + (k16 - 4) * 512, 512), d_out[k16 % 4], []
                t_o = stt(slot, ht[c][:, cols], cstv[:, 16 + c:17 + c], rs2[tl_], ALU.mult, ALU.mult,
                          after=[t_ex] + pre_)
                out_toks.append(P.dma("sp", oT[c * 128:(c + 1) * 128, cols], slot, dsem_, after=[t_o]))
        best = {}
        for t_ in out_toks:
            if id(t_.sem) not in best or best[id(t_.sem)].val < t_.val:
                best[id(t_.sem)] = t_
        P.wait("sp", list(best.values()))


_NC_CACHE = {}


def _host_consts(hf):
    f32 = np.float32
    inv_freq = 500000.0 ** (-(np.arange(0, 16, 2, dtype=np.float64)) / 16.0)
    pos = np.concatenate([np.arange(2048) + (hf - 1) * 2048, np.arange(2048) + hf * 2048]).astype(np.float64)
    ang = pos[None, :] * inv_freq[:, None]
    cs, sn = np.cos(ang).astype(f32), np.sin(ang).astype(f32)
    tab = np.zeros((128, 8192), f32)
    for p in range(128):
        dim = p % 64
        if dim < 8:
            tab[p, 0:4096] = cs[dim]
            tab[p, 4096:] = -sn[dim]
        elif dim < 16:
            tab[p, 0:4096] = cs[dim - 8]
            tab[p, 4096:] = sn[dim - 8]
        else:
            tab[p, 0:4096] = 1.0
    k = np.arange(128)[:, None]
    q = np.arange(128)[None, :]
    mprev = (q <= k).astype(f32)
    mdiag = (q >= k).astype(f32)
    msk = np.concatenate([mprev, mdiag, mprev, mdiag, mprev * hf, mdiag, mprev * hf, mdiag], axis=1).astype(f32)
    prm = np.zeros((128, 128), f32)
    for m in range(128):
        dim = m % 64
        if dim < 8:
            prm[m + 8, m] = 1.0
        elif dim < 16:
            prm[m - 8, m] = 1.0
    return tab, msk, prm


def make_in_maps(x, norm_mix, w_in, w_pool_group, pool_scale, w_att_branch, w_pool_branch,
                 w_out, norm_ffn, w_ffn_gate, w_ffn_up, w_ffn_down, norm_final):
    f32 = np.float32
    x = np.asarray(x, f32)
    w_in0 = np.asarray(w_in, f32)[0]
    cols = []
    for sp in range(4):
        for g in range(3):
            for typ in range(3):
                base = typ * 1536 + g * 512 + sp * 128
                cols.append(np.arange(base, base + 128))
    cols.append(np.arange(4608, 7168))
    cols = np.concatenate(cols)
    w_in_r = np.ascontiguousarray(w_in0[:, cols])
    wg = np.asarray(w_ffn_gate, f32)[0].reshape(D, NJ, 128)
    wu = np.asarray(w_ffn_up, f32)[0].reshape(D, NJ, 128)
    wgu = np.ascontiguousarray(np.stack([wg, wu], axis=2).reshape(D, 2 * DFF))
    wd = np.ascontiguousarray(np.asarray(w_ffn_down, f32)[0])
    wpg = np.ascontiguousarray(np.asarray(w_pool_group, f32)[0].transpose(1, 0, 2).reshape(128, 512))
    wab = np.ascontiguousarray(np.asarray(w_att_branch, f32)[0])
    wpb = np.ascontiguousarray(np.asarray(w_pool_branch, f32)[0])
    wout = np.ascontiguousarray(np.asarray(w_out, f32)[0])

    in_maps = []
    for c in range(8):
        b, hf = c // 2, c % 2
        xT = np.zeros((D, 4096), f32)
        xT[:, 2048:] = x[b, hf * 2048:(hf + 1) * 2048, :].T
        if hf == 1:
            xT[:, :2048] = x[b, 0:2048, :].T
        tab, msk, prm = _host_consts(hf)
        cst = np.zeros((128, 128), f32)
        cst[:, 0:8] = np.asarray(norm_mix, f32)[0].reshape(8, 128).T
        cst[:, 8:16] = np.asarray(norm_ffn, f32)[0].reshape(8, 128).T
        cst[:, 16:24] = np.asarray(norm_final, f32).reshape(8, 128).T
        cst[:, 24:28] = np.asarray(pool_scale, f32)[0].reshape(4, 128).T
        cst[:, 96] = EPS
        for gz in range(4):
            w = 2 ** (gz + 1)
            posj = hf * 2048 + np.arange(16)
            cnt = np.minimum(posj + 1, w).astype(f32)
            cst[:, 28 + gz * 16:28 + gz * 16 + 16] = (f32(1.0) / cnt)[None, :]
        in_maps.append({"xT": xT, "w_in": w_in_r, "tab": tab, "cst": cst, "msk": msk, "prm": prm,
                        "wpg": wpg, "wab": wab, "wpb": wpb, "wout": wout, "wgu": wgu, "wd": wd})
    return in_maps


def kernel(x, norm_mix, w_in, w_pool_group, pool_scale, w_att_branch, w_pool_branch,
           w_out, norm_ffn, w_ffn_gate, w_ffn_up, w_ffn_down, norm_final):
    f32 = np.float32
    in_maps = make_in_maps(x, norm_mix, w_in, w_pool_group, pool_scale, w_att_branch, w_pool_branch,
                           w_out, norm_ffn, w_ffn_gate, w_ffn_up, w_ffn_down, norm_final)
    if "nc" not in _NC_CACHE:
        _NC_CACHE["nc"] = build_program()
    res = run_bass_kernel_spmd(_NC_CACHE["nc"], in_maps, core_ids=list(range(8)))
    out = np.empty((4, S, D), f32)
    for c in range(8):
        b, hf = c // 2, c % 2
        out[b, hf * 2048:(hf + 1) * 2048, :] = res.results[c]["oT"].T
    return out
```
